# Optimizing a Trainium2 kernel written in Bass

```python
import jax, jax.numpy as jnp
from jax import lax
import numpy as np

D_MODEL = 1024
BATCH = 8
SEQ = 2048
DEPTH = 4
DEC_BATCH = 32
DEC_SEQ = 4
PAST_LEN = 8192
PAGE_SIZE = 128

N_HEADS = 16
HEAD_DIM = D_MODEL // N_HEADS
D_FF = 4 * D_MODEL
CONV_WIDTH = 31
N_A = DEPTH // 2
N_B = DEPTH - N_A
Q_BLOCK = 128
LN_EPS = 1e-5
DEEPNORM_ALPHA = (2.0 * DEPTH) ** 0.25
DEEPNORM_BETA = (8.0 * DEPTH) ** -0.25
BIAS_HI = -4.0
BIAS_LO = -9.0

kernel_name = 'yoco_conformer_stickbreaking_decoder_step'


def layer_norm(x, g, b):
    xf = x.astype(jnp.float32)
    mu = jnp.mean(xf, axis=-1, keepdims=True)
    var = jnp.mean(jnp.square(xf - mu), axis=-1, keepdims=True)
    y = (xf - mu) * lax.rsqrt(var + LN_EPS)
    return (y * g.astype(jnp.float32) + b.astype(jnp.float32)).astype(x.dtype)


def ada_mod(c, w, b):
    m = jax.nn.silu(c) @ w + b
    shift, scale, gate = jnp.split(m[:, None, :], 3, axis=-1)
    return shift, scale, gate


def post_norm_residual(x, y, gate, g, b):
    return layer_norm(DEEPNORM_ALPHA * x + (1.0 + gate) * y, g, b)


def conformer_conv(h, conv_prev, pw1_w, pw1_b, dw_w, dw_b, ln_g, ln_b, pw2_w):
    a, g = jnp.split(h @ pw1_w + pw1_b, 2, axis=-1)
    u = a * jax.nn.sigmoid(g)
    full = jnp.concatenate([conv_prev, u], axis=1)
    y = lax.conv_general_dilated(full, dw_w[:, None, :], (1,), 'VALID',
                                 dimension_numbers=('NWC', 'WIO', 'NWC'),
                                 feature_group_count=D_MODEL) + dw_b
    y = jax.nn.silu(layer_norm(y, ln_g, ln_b))
    return y @ pw2_w, full[:, -(CONV_WIDTH - 1):]


def stick_breaking(q, k, v, bias, q_pos, k_pos):
    z = jnp.einsum('bqhd,bkhd->bhqk', q, k).astype(jnp.float32) * (HEAD_DIM ** -0.5)
    z = z + bias.astype(jnp.float32)[None, :, None, None]
    mask = k_pos[None, :] < q_pos[:, None]
    log_rem = jnp.where(mask, jax.nn.log_sigmoid(-z), 0.0)
    suffix = lax.cumsum(log_rem, axis=3, reverse=True)
    after = jnp.concatenate([suffix[..., 1:], jnp.zeros_like(suffix[..., :1])], axis=-1)
    w = jnp.where(mask, jnp.exp(jax.nn.log_sigmoid(z) + after), 0.0)
    return jnp.einsum('bhqk,bkhd->bqhd', w.astype(v.dtype), v)


def stick_breaking_prompt(q, k, v, bias):
    B, T = q.shape[0], q.shape[1]
    nb = T // Q_BLOCK
    qb = q.reshape(B, nb, Q_BLOCK, N_HEADS, HEAD_DIM).transpose(1, 0, 2, 3, 4)
    pos = jnp.arange(T, dtype=jnp.int32)
    qpos = pos.reshape(nb, Q_BLOCK)
    ob = lax.map(lambda a: stick_breaking(a[0], k, v, bias, a[1], pos), (qb, qpos))
    return ob.transpose(1, 0, 2, 3, 4).reshape(B, T, N_HEADS, HEAD_DIM)


def run_trunk(x, c, conv_prev, k_past, v_past, ada_w, ada_b, ln_g, ln_b, mlp_w1, mlp_w2,
              conv_pw1_w, conv_pw1_b, conv_dw_w, conv_dw_b, conv_ln_g, conv_ln_b, conv_pw2_w,
              w_kv, attn_wq, attn_wo, attn_bias):
    B, T = x.shape[0], x.shape[1]
    conv_new = []
    k_new = v_new = k_all = v_all = None
    for l in range(DEPTH):
        shift, scale, gate = ada_mod(c, ada_w[l, 0], ada_b[l, 0])
        h = x * (1.0 + scale) + shift
        if l < N_A:
            y, st = conformer_conv(h, conv_prev[l], conv_pw1_w[l], conv_pw1_b[l], conv_dw_w[l],
                                   conv_dw_b[l], conv_ln_g[l], conv_ln_b[l], conv_pw2_w[l])
            conv_new.append(st)
        else:
            if l == N_A:
                k_flat, v_flat = jnp.split(x @ w_kv, 2, axis=-1)
                k_new = k_flat.reshape(B, T, N_HEADS, HEAD_DIM)
                v_new = v_flat.reshape(B, T, N_HEADS, HEAD_DIM)
                if k_past is None:
                    k_all, v_all = k_new, v_new
                else:
                    k_all = jnp.concatenate([k_past, k_new], axis=1)
                    v_all = jnp.concatenate([v_past, v_new], axis=1)
            j = l - N_A
            q = (h @ attn_wq[j]).reshape(B, T, N_HEADS, HEAD_DIM)
            if k_past is None:
                o = stick_breaking_prompt(q, k_all, v_all, attn_bias[j])
            else:
                p = k_past.shape[1]
                q_pos = p + jnp.arange(T, dtype=jnp.int32)
                k_pos = jnp.arange(p + T, dtype=jnp.int32)
                o = stick_breaking(q, k_all, v_all, attn_bias[j], q_pos, k_pos)
            y = o.reshape(B, T, D_MODEL) @ attn_wo[j]
        x = post_norm_residual(x, y, gate, ln_g[l, 0], ln_b[l, 0])
        shift, scale, gate = ada_mod(c, ada_w[l, 1], ada_b[l, 1])
        h = x * (1.0 + scale) + shift
        y = jnp.square(jax.nn.relu(h @ mlp_w1[l])) @ mlp_w2[l]
        x = post_norm_residual(x, y, gate, ln_g[l, 1], ln_b[l, 1])
    return x, jnp.stack(conv_new, axis=0), k_new, v_new


def setup_inputs(seed: int = 0) -> dict:
    key = jax.random.key(seed)
    ks = jax.random.split(key, 32)
    n_pages = PAST_LEN // PAGE_SIZE
    n_used = DEC_BATCH * n_pages
    n_pool = (n_used * 5) // 4
    f = jnp.float32
    dsc = D_MODEL ** -0.5
    x_prompt = jax.random.normal(ks[0], (BATCH, SEQ, D_MODEL), f)
    x_sample = jax.random.normal(ks[1], (DEC_BATCH, DEC_SEQ, D_MODEL), f)
    c_prompt = jax.random.normal(ks[2], (BATCH, D_MODEL), f)
    c_sample = jax.random.normal(ks[3], (DEC_BATCH, D_MODEL), f)
    state_conv = 0.5 * jax.random.normal(ks[4], (N_A, DEC_BATCH, CONV_WIDTH - 1, D_MODEL), f)
    cache_k = jax.random.normal(ks[5], (n_pool, PAGE_SIZE, N_HEADS, HEAD_DIM), f)
    cache_v = jax.random.normal(ks[6], (n_pool, PAGE_SIZE, N_HEADS, HEAD_DIM), f)
    page_table = jax.random.permutation(ks[7], n_pool)[:n_used].reshape(DEC_BATCH, n_pages).astype(jnp.int32)
    ada_w = jax.random.normal(ks[8], (DEPTH, 2, D_MODEL, 3 * D_MODEL), f) * dsc
    ada_b = 0.02 * jax.random.normal(ks[9], (DEPTH, 2, 3 * D_MODEL), f)
    ln_g = 1.0 + 0.02 * jax.random.normal(ks[10], (DEPTH, 2, D_MODEL), f)
    ln_b = 0.02 * jax.random.normal(ks[11], (DEPTH, 2, D_MODEL), f)
    mlp_w1 = jax.random.normal(ks[12], (DEPTH, D_MODEL, D_FF), f) * dsc
    mlp_w2 = jax.random.normal(ks[13], (DEPTH, D_FF, D_MODEL), f) * (D_FF ** -0.5) * DEEPNORM_BETA
    conv_pw1_w = jax.random.normal(ks[14], (N_A, D_MODEL, 2 * D_MODEL), f) * dsc
    conv_pw1_b = 0.02 * jax.random.normal(ks[15], (N_A, 2 * D_MODEL), f)
    conv_dw_w = jax.random.normal(ks[16], (N_A, CONV_WIDTH, D_MODEL), f) * (CONV_WIDTH ** -0.5)
    conv_dw_b = 0.02 * jax.random.normal(ks[17], (N_A, D_MODEL), f)
    conv_ln_g = 1.0 + 0.02 * jax.random.normal(ks[18], (N_A, D_MODEL), f)
    conv_ln_b = 0.02 * jax.random.normal(ks[19], (N_A, D_MODEL), f)
    conv_pw2_w = jax.random.normal(ks[20], (N_A, D_MODEL, D_MODEL), f) * dsc * DEEPNORM_BETA
    w_kv = jax.random.normal(ks[21], (D_MODEL, 2 * D_MODEL), f) * dsc
    attn_wq = jax.random.normal(ks[22], (N_B, D_MODEL, D_MODEL), f) * dsc
    attn_wo = jax.random.normal(ks[23], (N_B, D_MODEL, D_MODEL), f) * dsc * DEEPNORM_BETA
    attn_bias = BIAS_LO + (BIAS_HI - BIAS_LO) * jax.random.uniform(ks[24], (N_B, N_HEADS), f)
    return {'x_prompt': x_prompt, 'x_sample': x_sample, 'c_prompt': c_prompt, 'c_sample': c_sample,
            'state_conv': state_conv, 'cache_k': cache_k, 'cache_v': cache_v, 'page_table': page_table,
            'ada_w': ada_w, 'ada_b': ada_b, 'ln_g': ln_g, 'ln_b': ln_b, 'mlp_w1': mlp_w1, 'mlp_w2': mlp_w2,
            'conv_pw1_w': conv_pw1_w, 'conv_pw1_b': conv_pw1_b, 'conv_dw_w': conv_dw_w, 'conv_dw_b': conv_dw_b,
            'conv_ln_g': conv_ln_g, 'conv_ln_b': conv_ln_b, 'conv_pw2_w': conv_pw2_w,
            'w_kv': w_kv, 'attn_wq': attn_wq, 'attn_wo': attn_wo, 'attn_bias': attn_bias}


def reference(x_prompt, x_sample, c_prompt, c_sample, state_conv, cache_k, cache_v, page_table,
              ada_w, ada_b, ln_g, ln_b, mlp_w1, mlp_w2, conv_pw1_w, conv_pw1_b, conv_dw_w, conv_dw_b,
              conv_ln_g, conv_ln_b, conv_pw2_w, w_kv, attn_wq, attn_wo, attn_bias):
    conv_zero = jnp.zeros((N_A, x_prompt.shape[0], CONV_WIDTH - 1, D_MODEL), x_prompt.dtype)
    y_prompt, new_conv_prompt, new_k_prompt, new_v_prompt = run_trunk(
        x_prompt, c_prompt, conv_zero, None, None, ada_w, ada_b, ln_g, ln_b, mlp_w1, mlp_w2,
        conv_pw1_w, conv_pw1_b, conv_dw_w, conv_dw_b, conv_ln_g, conv_ln_b, conv_pw2_w,
        w_kv, attn_wq, attn_wo, attn_bias)
    db, n_pages = page_table.shape
    past_len = n_pages * cache_k.shape[1]
    k_past = cache_k[page_table].reshape(db, past_len, N_HEADS, HEAD_DIM)
    v_past = cache_v[page_table].reshape(db, past_len, N_HEADS, HEAD_DIM)
    y_sample, new_conv_sample, new_k_sample, new_v_sample = run_trunk(
        x_sample, c_sample, state_conv, k_past, v_past, ada_w, ada_b, ln_g, ln_b, mlp_w1, mlp_w2,
        conv_pw1_w, conv_pw1_b, conv_dw_w, conv_dw_b, conv_ln_g, conv_ln_b, conv_pw2_w,
        w_kv, attn_wq, attn_wo, attn_bias)
    return (y_prompt, y_sample, new_conv_prompt, new_conv_sample, new_k_prompt, new_v_prompt, new_k_sample, new_v_sample)
```

```python
import contextlib
import numpy as np
import concourse.bass as bass
import concourse.mybir as mybir
from concourse.bass_utils import run_bass_kernel_spmd

F32 = mybir.dt.float32
BF16 = mybir.dt.bfloat16
I32 = mybir.dt.int32
AF = mybir.ActivationFunctionType
ALU = mybir.AluOpType

ALPHA = 8.0 ** 0.25
LN_EPS = 1e-5
NEG = -240000.0
ENGS = ("pe", "act", "dve", "pool", "sp")
import os as _os
SAME_ENGINE_SYNC = bool(int(_os.environ.get("KSES", "0")))


def I(method, *args, **kwargs):
    return (method, args, kwargs)


class Prog:
    def __init__(self, nc):
        self.nc = nc
        self.ops = {e: [] for e in ENGS}
        self.nsig = {e: 0 for e in ENGS}
        self.pending = {e: False for e in ENGS}
        self.last_w = {}
        self.readers = {}
        self.dma_cnt = {}
        self.waited = {e: {} for e in ENGS}
        self.extra = {e: [] for e in ENGS}
        self.small_ev = {}

    def _deps(self, eng, reads, writes, skip_sem=None, small=True):
        deps = {}
        need_same = [small]

        def add(ev):
            if ev is None:
                return
            s, v = ev
            if s == eng and self.small_ev.get((s, v), True):
                need_same[0] = True
            if deps.get(s, 0) < v:
                deps[s] = v

        for k in reads:
            add(self.last_w.get(k))
        for k in writes:
            add(self.last_w.get(k))
            for s, v in self.readers.get(k, {}).items():
                add((s, v))
        for ev in self.extra[eng]:
            add(ev)
        self.extra[eng] = []
        out = []
        for s, v in deps.items():
            if s == skip_sem:
                continue
            if s == eng:
                if eng == "pe":
                    continue
                if not SAME_ENGINE_SYNC and not need_same[0]:
                    continue
                if v > self.nsig[eng]:
                    continue
            if self.waited[eng].get(s, 0) >= v:
                continue
            self.waited[eng][s] = v
            out.append((s, v))
        return out

    def _register(self, ev, reads, writes):
        for k in reads:
            r = self.readers.setdefault(k, {})
            if r.get(ev[0], 0) < ev[1]:
                r[ev[0]] = ev[1]
        for k in writes:
            self.last_w[k] = ev
            self.readers[k] = {}

    def op(self, eng, fn, reads=(), writes=(), signal=True):
        small = True
        if eng != "pe":
            o = fn[2].get("out", fn[1][0] if fn[1] else None)
            try:
                small = o is None or o.free_size() < 256
            except Exception:
                small = True
        waits = self._deps(eng, reads, writes, small=small)
        ev = (eng, self.nsig[eng] + 1)
        self.small_ev[ev] = small
        inc = None
        if signal:
            self.nsig[eng] += 1
            inc = (eng, 1)
            self.pending[eng] = False
        else:
            self.pending[eng] = True
        self._register(ev, reads, writes)
        self.ops[eng].append((waits, fn, inc))

    def raw(self, eng, fn):
        self.ops[eng].append(([], fn, None))

    def dma(self, q, sem, fn, reads=(), writes=()):
        waits = self._deps(q, reads, writes, skip_sem=sem)
        self.dma_cnt[sem] = self.dma_cnt.get(sem, 0) + 1
        ev = (sem, 16 * self.dma_cnt[sem])
        self._register(ev, reads, writes)
        self.ops[q].append((waits, fn, (sem, 16)))

    def barrier(self):
        evs = [(e, self.nsig[e]) for e in ENGS if self.nsig[e] > 0]
        evs += [(s, 16 * c) for s, c in self.dma_cnt.items() if not s.startswith("d_w")]
        for e in ENGS:
            assert not self.pending[e]
            self.extra[e] = list(evs)

    def emit(self):
        nc = self.nc
        for e in ENGS:
            assert not self.pending[e], e
        names = list(ENGS) + sorted(self.dma_cnt.keys())
        fin = [(e, self.nsig[e]) for e in ENGS if e != "sp" and self.nsig[e] > 0]
        fin += [(s, 16 * c) for s, c in self.dma_cnt.items()]
        with contextlib.ExitStack() as st:
            sems = {n: st.enter_context(nc.semaphore("s_" + n)) for n in names}
            block = st.enter_context(nc.Block())

            def run(name, eng):
                for waits, fn, inc in self.ops[name]:
                    for s, v in waits:
                        eng.wait_ge(sems[s], v)
                    ins = getattr(eng, fn[0])(*fn[1], **fn[2])
                    if inc is not None:
                        ins.then_inc(sems[inc[0]], inc[1])

            @block.tensor
            def _(e):
                run("pe", e)

            @block.scalar
            def _(e):
                run("act", e)

            @block.vector
            def _(e):
                run("dve", e)

            @block.gpsimd
            def _(e):
                run("pool", e)

            @block.sync
            def _(e):
                run("sp", e)
                for s, v in fin:
                    e.wait_ge(sems[s], v)


VOFF = {}
_o = 0
for _n, _sz in (("adab", 192), ("lng", 64), ("lnb", 64), ("pw1b", 32), ("dww", 496), ("dwb", 16),
                ("clg", 16), ("clb", 16), ("ct", 40)):
    VOFF[_n] = _o
    _o += _sz
NVEC = _o


def build_program(npool=2560, N_LAYERS_RUN=4, passes=(0, 1), do_sample_attn=True, do_prompt_attn=True):
    nc = bass.Bass("TRN2", target_bir_lowering=False)

    def din(name, shape, dt=F32):
        return nc.dram_tensor(name, list(shape), dt, kind="ExternalInput").ap()

    def dout(name, shape, dt=F32):
        return nc.dram_tensor(name, list(shape), dt, kind="ExternalOutput").ap()

    xp = din("xp", [128, 8, 2048])
    xs = din("xs", [128, 8, 16])
    vec = din("vec", [128, NVEC])
    stc = din("stc", [128, 2, 8, 4, 30])
    ck = din("ck", [npool * 128, 1024])
    cv = din("cv", [npool * 128, 1024])
    ptab = din("ptab", [4, 64], I32)
    abias = din("abias", [1, 32])
    adaw = din("adaw", [8, 1024, 3072])
    w1 = din("w1", [4, 1024, 4096])
    w2 = din("w2", [4, 4096, 1024])
    pw1 = din("pw1", [2, 1024, 2048])
    pw2 = din("pw2", [2, 1024, 1024])
    wkv = din("wkv", [1024, 2048])
    wq = din("wq", [2, 1024, 1024])
    wo = din("wo", [2, 1024, 1024])

    yp = dout("yp", [128, 8, 2048])
    ys = dout("ys", [128, 8, 16])
    ncp = dout("ncp", [128, 2, 8, 30])
    ncs = dout("ncs", [128, 2, 8, 4, 30])
    nkp = dout("nkp", [2048, 1024])
    nvp = dout("nvp", [2048, 1024])
    nks = dout("nks", [16, 1024])
    nvs = dout("nvs", [16, 1024])

    P = Prog(nc)
    with contextlib.ExitStack() as st:
        def sb(name, shape, dt):
            return st.enter_context(nc.sbuf_tensor(name, list(shape), dt))

        X = sb("X", [128, 8, 1040], F32)
        H = sb("H", [128, 8, 1040], BF16)
        KTA = sb("KTA", [128, 8, 1024], BF16)
        VA = sb("VA", [128, 8, 1024], BF16)
        RB = sb("RB", [128, 8768], F32)
        WB = [sb("WB0", [128, 4096], BF16), sb("WB1", [128, 4096], BF16)]
        R1 = sb("R1", [128, 11900], F32)
        MOD = sb("MOD", [128, 8, 24, 5], F32)
        VEC = sb("VEC", [128, NVEC], F32)
        IDENT = sb("IDENT", [128, 128], BF16)
        IDENTN = sb("IDENTN", [128, 128], BF16)
        TRI = sb("TRI", [128, 128], BF16)
        ONES = sb("ONES", [128, 128], BF16)
        ONESS = sb("ONESS", [128, 128], BF16)
        CFILL = sb("CFILL", [128, 512], BF16)
        MASK = sb("MASK", [128, 4, 512], BF16)
        BIAS = sb("BIAS", [128, 32], F32)
        TAIL = sb("TAIL", [128, 2, 8, 30], F32)
        SCT = sb("SCT", [128, 8, 5], BF16)
        STG = [sb("STG0", [128, 512], F32), sb("STG1", [128, 512], F32)]
        MASKN = sb("MASKN", [16, 4, 64], F32)
        ZF = sb("ZF", [16, 64], F32)
        BM = sb("BM", [128, 2, 4, 64], F32)
        QB = sb("QB", [128, 8, 8], BF16)
        PTB = sb("PTB", [128, 256], I32)
        PTF = sb("PTF", [128, 256], F32)
        IOP = sb("IOP", [128, 1], F32)
        IDX = sb("IDX", [128, 256], I32)
        G16 = sb("G16", [128, 8, 8, 16], F32)
        TMP16 = sb("TMP16", [128, 2, 16], F32)
        PS = [st.enter_context(nc.psum_tensor("PS%d" % i, [128, 512], F32)) for i in range(8)]
        PST = PS[7][:, :].bitcast(BF16)

        KTB = RB[:, 0:4160].bitcast(BF16).rearrange("p (c n) -> p c n", c=8)
        VB = RB[:, 4160:8256].bitcast(BF16).rearrange("p (c n) -> p c n", c=8)
        VN = RB[:, 8256:8768].bitcast(BF16)
        YC = RB[:, 0:8320].rearrange("p (c n) -> p c n", c=8)

        def r1(off, n, dt=F32):
            nf = n if dt == F32 else (n + 1) // 2
            v = R1[:, off:off + nf]
            if dt != F32:
                v = v.bitcast(dt)
            return v, off + nf

        o = 0
        UB = []
        for i in range(2):
            v, o = r1(o, 1054)
            UB.append(v)
        USB = []
        for i in range(2):
            v, o = r1(o, 136)
            USB.append(v.rearrange("p (s n) -> p s n", s=4))
        SIG = []
        for i in range(2):
            v, o = r1(o, 512)
            SIG.append(v)
        o = 0
        HID = []
        for i in range(2):
            v, o = r1(o, 4 * 1040, BF16)
            HID.append(v.rearrange("p (c n) -> p c n", c=4))
        RL = []
        for i in range(2):
            v, o = r1(o, 512)
            RL.append(v)
        o = 0
        TBs, TSs = [], []
        for i in range(2):
            v, o = r1(o, 1040, BF16)
            TBs.append(v)
        for i in range(2):
            v, o = r1(o, 1040, BF16)
            TSs.append(v)
        MSs, RSs = [], []
        for i in range(3):
            v, o = r1(o, 512)
            MSs.append(v)
        for i in range(3):
            v, o = r1(o, 512)
            RSs.append(v)
        LT = []
        for i in range(2):
            v, o = r1(o, 512)
            LT.append(v)
        o = 0
        Q, o = r1(o, 8 * 1040, BF16)
        Q = Q.rearrange("p (c n) -> p c n", c=8)
        o_attn = o
        QN = []
        for i in range(2):
            v, o = r1(o, 1024, BF16)
            QN.append(v)
        ET = []
        for i in range(2):
            v, o = r1(o, 512)
            ET.append(v)
        SPT = []
        for i in range(4):
            v, o = r1(o, 512, BF16)
            SPT.append(v)
        WT = []
        for i in range(4):
            v, o = r1(o, 512, BF16)
            WT.append(v)
        STs = []
        for i in range(2):
            v, o = r1(o, 512, BF16)
            STs.append(v)
        assert o <= 11900, o
        o = o_attn
        KBs = []
        for i in range(3):
            v, o = r1(o, 1024, BF16)
            KBs.append(v)
        VBs = []
        for i in range(8):
            v, o = r1(o, 1024, BF16)
            VBs.append(v)
        KTps = []
        for i in range(1):
            v, o = r1(o, 1024, BF16)
            KTps.append(v)
        ZBt, o = r1(o, 256)
        ETs, o = r1(o, 256)
        TMt, o = r1(o, 256)
        SPb, o = r1(o, 256, BF16)
        Wb, o = r1(o, 256, BF16)
        Stl, o = r1(o, 256)
        Sb, o = r1(o, 256, BF16)
        SC, o = r1(o, 64)
        OACC, o = r1(o, 64)
        assert o <= 11900, o

        plan = []

        def blk1024(wap, col0):
            def f(slot):
                dst = WB[slot][:, :].rearrange("p (k n) -> p k n", k=8)
                src = wap[:, col0:col0 + 512].rearrange("(k p) n -> p k n", p=128)
                return [I("dma_start", out=dst, in_=src)]
            return f

        def blkw2(l, j):
            def f(slot):
                dst = WB[slot][:, :].rearrange("p (k n) -> p k n", k=4)
                src = w2[l][j * 512:(j + 1) * 512, :].rearrange("(k p) n -> p k n", p=128)
                return [I("dma_start", out=dst, in_=src)]
            return f

        def blkpw1(l, b):
            def f(slot):
                dst = WB[slot][:, :].rearrange("p (k n) -> p k n", k=8)
                sa = pw1[l][:, b * 256:(b + 1) * 256].rearrange("(k p) n -> p k n", p=128)
                sg = pw1[l][:, 1024 + b * 256:1024 + (b + 1) * 256].rearrange("(k p) n -> p k n", p=128)
                return [I("dma_start", out=dst[:, :, 0:256], in_=sa),
                        I("dma_start", out=dst[:, :, 256:512], in_=sg)]
            return f

        for ls in range(2):
            for b in range(6):
                plan.append((("ada", ls, b), blk1024(adaw[ls], b * 512)))

        def ada_gap(pi, l, g):
            if pi != passes[0] or l + 1 >= 4 or g >= 12:
                return None
            return (2 * (l + 1) + g // 6, g % 6)
        for pi in passes:
            for l in range(N_LAYERS_RUN):
                if l < 2:
                    for b in range(4):
                        plan.append((("pw1", l, b), blkpw1(l, b)))
                    for b in range(2):
                        plan.append((("pw2", l, b), blk1024(pw2[l], b * 512)))
                else:
                    if l == 2:
                        for b in range(4):
                            plan.append((("wkv", b), blk1024(wkv, b * 512)))
                    for b in range(2):
                        plan.append((("wq", l - 2, b), blk1024(wq[l - 2], b * 512)))
                    for b in range(2):
                        plan.append((("wo", l - 2, b), blk1024(wo[l - 2], b * 512)))
                for j in range(8):
                    plan.append((("w1", l, j), blk1024(w1[l], j * 512)))
                    ag = ada_gap(pi, l, 2 * j)
                    if ag:
                        plan.append((("ada", ag[0], ag[1]), blk1024(adaw[ag[0]], ag[1] * 512)))
                    plan.append((("w2", l, j), blkw2(l, j)))
                    ag = ada_gap(pi, l, 2 * j + 1)
                    if ag:
                        plan.append((("ada", ag[0], ag[1]), blk1024(adaw[ag[0]], ag[1] * 512)))

        wst = {"issue": 0, "use": 0}

        def w_issue():
            i = wst["issue"]
            if i >= len(plan):
                return
            slot = i % 2
            for fn in plan[i][1](slot):
                P.dma("pool", "d_w%d" % slot, fn, writes=["W%d" % slot])
            wst["issue"] += 1

        def w_acquire(tag):
            i = wst["use"]
            assert plan[i][0] == tag, (plan[i][0], tag)
            slot = i % 2
            return WB[slot], "W%d" % slot

        def w_release():
            wst["use"] += 1
            w_issue()

        bank_rot = {"i": 0}
        tmp16_rot = {"i": 0}

        def nb(n=6):
            b = bank_rot["i"] % n
            bank_rot["i"] += 1
            return b

        alt = {"i": 0}

        def evac_engine():
            alt["i"] += 1
            return "act" if alt["i"] % 2 else "dve"

        def copy_op(eng, out, in_, reads, writes):
            if eng == "act":
                P.op("act", I("activation", out=out, in_=in_, func=AF.Identity), reads=reads, writes=writes)
            else:
                P.op(eng, I("tensor_copy", out=out, in_=in_), reads=reads, writes=writes)

        def V_(name, i=0, n=1):
            o = VOFF[name] + i
            return VEC[:, o:o + n]

        P.dma("sp", "d_vec", I("dma_start", out=VEC[:, :], in_=vec), writes=["VEC"])
        P.dma("sp", "d_bias", I("dma_start", out=BIAS[:, :], in_=abias[0:1, :].partition_broadcast(128)), writes=["BIAS"])
        P.dma("sp", "d_pt", I("dma_start", out=PTB[:, :], in_=ptab.rearrange("(o s) n -> o (s n)", o=1).partition_broadcast(128)), writes=["PTB"])
        w_issue()
        w_issue()
        P.op("pool", I("memset", ONES[:, :], 1.0), writes=["ONES"])
        P.op("pool", I("memset", ONESS[:, :], 1.0 / 1024.0), writes=["ONESS"])
        P.op("pool", I("memset", CFILL[:, :], 0.0), writes=["CFILL"])
        P.op("pool", I("affine_select", out=TRI[:, :], in_=ONES[:, :], pattern=[[-1, 128]], compare_op=ALU.is_ge, fill=0.0, base=0, channel_multiplier=1), reads=["ONES"], writes=["TRI"])
        P.op("pool", I("affine_select", out=IDENT[:, :], in_=ONES[:, :], pattern=[[-1, 128]], compare_op=ALU.is_equal, fill=0.0, base=0, channel_multiplier=1), reads=["ONES"], writes=["IDENT"])
        P.op("pool", I("tensor_scalar", out=IDENTN[:, :], in0=IDENT[:, :], scalar1=-0.125, scalar2=None, op0=ALU.mult), reads=["IDENT"], writes=["IDENTN"])
        for r in range(4):
            P.op("pool", I("affine_select", out=MASK[:, r, :], in_=CFILL[:, :], pattern=[[1, 512]], compare_op=ALU.is_gt, fill=NEG, base=-128 * r, channel_multiplier=-1), reads=["CFILL"], writes=["MASK"])
        P.op("pool", I("memset", ZF[:, :], 0.0), writes=["ZF"])
        for i in range(4):
            P.op("pool", I("affine_select", out=MASKN[:, i, :], in_=ZF[:, :], pattern=[[0, 64]], compare_op=ALU.is_ge, fill=-30000.0, base=-4 * i, channel_multiplier=1), reads=["ZF"], writes=["MASKN"])
            P.op("pool", I("affine_select", out=MASKN[:, i, :], in_=MASKN[:, i, :], pattern=[[0, 16], [1, 4]], compare_op=ALU.is_gt, fill=-30000.0, base=4 * i, channel_multiplier=-1), reads=["MASKN"], writes=["MASKN"])
        P.op("pool", I("memset", QB[:, :, :], 0.0), writes=["QB"])
        P.op("pool", I("iota", IOP[:, :], pattern=[[0, 1]], base=0, channel_multiplier=1, allow_small_or_imprecise_dtypes=True), writes=["IOP"])
        P.op("dve", I("tensor_copy", out=PTF[:, :], in_=PTB[:, :]), reads=["PTB"], writes=["PTF"])
        P.op("dve", I("tensor_scalar", out=PTF[:, :], in0=PTF[:, :], scalar1=128.0, scalar2=IOP[:, 0:1], op0=ALU.mult, op1=ALU.add), reads=["PTF", "IOP"], writes=["PTF"])
        P.op("dve", I("tensor_copy", out=IDX[:, :], in_=PTF[:, :]), reads=["PTF"], writes=["IDX"])

        P.op("act", I("activation", out=SCT[:, :, :], in_=V_("ct", 0, 40).rearrange("p (k s) -> p k s", k=8), func=AF.Silu), reads=["VEC"], writes=["SCT"])
        def ada_block(ls, b):
            wv, wk = w_acquire(("ada", ls, b))
            wv3 = wv[:, :].rearrange("p (k n) -> p k n", k=8)
            bk = b % 2
            for oc in range(4):
                for kc in range(8):
                    P.op("pe", I("matmul", PS[bk][:, oc * 5:oc * 5 + 5], lhsT=wv3[:, kc, oc * 128:(oc + 1) * 128], rhs=SCT[:, kc, :], start=(kc == 0), stop=(kc == 7)),
                         reads=[wk, "SCT"], writes=["ps%d" % bk], signal=(oc == 3 and kc == 7))
            for oc in range(4):
                og = b * 4 + oc
                P.op("dve", I("tensor_scalar", out=MOD[:, ls, og, :], in0=PS[bk][:, oc * 5:oc * 5 + 5], scalar1=V_("adab", ls * 24 + og), scalar2=None, op0=ALU.add),
                     reads=["ps%d" % bk, "VEC"], writes=["MOD%d" % ls])
            w_release()
            if b == 5:
                P.op("dve", I("tensor_scalar", out=MOD[:, ls, 8:16, :], in0=MOD[:, ls, 8:16, :], scalar1=1.0, scalar2=None, op0=ALU.add), reads=["MOD%d" % ls], writes=["MOD%d" % ls])
                P.op("dve", I("tensor_scalar", out=MOD[:, ls, 16:24, :], in0=MOD[:, ls, 16:24, :], scalar1=1.0, scalar2=1.0 / ALPHA, op0=ALU.add, op1=ALU.mult), reads=["MOD%d" % ls], writes=["MOD%d" % ls])
                for sq in range(4):
                    for qq in range(4):
                        P.op("dve", I("tensor_copy", out=G16[:, ls, :, 4 * sq + qq], in_=MOD[:, ls, 16:24, 1 + sq]), reads=["MOD%d" % ls], writes=["G16_%d" % ls])

        for ls in range(2):
            for b in range(6):
                ada_block(ls, b)

        for pi in passes:
            NC = 1024 if pi == 0 else 1040
            tiles = [(0, 512), (512, 512)] + ([(1024, 16)] if pi == 1 else [])
            seqs = [(0, 0, 1024)] + ([(1 + i, 1024 + 4 * i, 4) for i in range(4)] if pi == 1 else [])
            tbase = 1024 * pi
            KT = KTA if pi == 0 else KTB
            VV = VA if pi == 0 else VB
            Xk = ["X%d" % c for c in range(8)]
            Hk = ["H%d" % c for c in range(8)]

            P.barrier()
            P.dma("sp", "d_x", I("dma_start", out=X[:, :, 0:1024], in_=xp[:, :, tbase:tbase + 1024]), writes=Xk)
            if pi == 1:
                P.dma("sp", "d_x", I("dma_start", out=X[:, :, 1024:1040], in_=xs), writes=Xk)

            def seq_ranges(c0, n):
                out = []
                for (si, s0, sn) in seqs:
                    a, b = max(c0, s0), min(c0 + n, s0 + sn)
                    if a < b:
                        out.append((si, a, b))
                return out

            def modulate(ls):
                for c in range(8):
                    for (si, s0, sn) in seqs:
                        P.op("act", I("activation", out=H[:, c, s0:s0 + sn], in_=X[:, c, s0:s0 + sn], func=AF.Identity, scale=MOD[:, ls, 8 + c, si:si + 1], bias=MOD[:, ls, c, si:si + 1]),
                             reads=[Xk[c], "MOD%d" % ls], writes=[Hk[c]])

            def resid_add(ls, oc, c0, n, bk):
                if c0 == 1024:
                    kq = tmp16_rot["i"] % 2
                    tmp16_rot["i"] += 1
                    P.op("dve", I("tensor_tensor", out=TMP16[:, kq, :], in0=PS[bk][:, 0:16], in1=G16[:, ls, oc, :], op=ALU.mult),
                         reads=["ps%d" % bk, "G16_%d" % ls], writes=["TMP16_%d" % kq])
                    P.op("dve", I("tensor_tensor", out=X[:, oc, 1024:1040], in0=X[:, oc, 1024:1040], in1=TMP16[:, kq, :], op=ALU.add),
                         reads=["TMP16_%d" % kq, Xk[oc]], writes=[Xk[oc]])
                    return
                for (si, a, b) in seq_ranges(c0, n):
                    P.op("dve", I("scalar_tensor_tensor", out=X[:, oc, a:b], in0=PS[bk][:, a - c0:b - c0], scalar=MOD[:, ls, 16 + oc, si:si + 1], in1=X[:, oc, a:b], op0=ALU.mult, op1=ALU.add),
                         reads=["ps%d" % bk, "MOD%d" % ls, Xk[oc]], writes=[Xk[oc]])

            def dense_to_resid(tag_fn, ls, src, srck):
                for ob in range(2):
                    wv, wk = w_acquire(tag_fn(ob))
                    wv3 = wv[:, :].rearrange("p (k n) -> p k n", k=8)
                    for o4 in range(4):
                        oc = ob * 4 + o4
                        for (c0, n) in tiles:
                            bk = nb()
                            for kc in range(8):
                                P.op("pe", I("matmul", PS[bk][:, 0:n], lhsT=wv3[:, kc, o4 * 128:(o4 + 1) * 128], rhs=src[:, kc, c0:c0 + n], start=(kc == 0), stop=(kc == 7)),
                                     reads=[wk] + srck, writes=["ps%d" % bk], signal=(kc == 7))
                            resid_add(ls, oc, c0, n, bk)
                    w_release()

            def layer_norm(buf, bufk, gname, bname, vi, eps, outfn):
                P.barrier()
                for c in range(8):
                    tb, ts = TBs[c % 2], TSs[c % 2]
                    P.op("act", I("activation", out=tb[:, 0:NC], in_=buf[:, c, 0:NC], func=AF.Identity), reads=[bufk[c]], writes=["TB%d" % (c % 2)])
                    P.op("act", I("activation", out=ts[:, 0:NC], in_=buf[:, c, 0:NC], func=AF.Square), reads=[bufk[c]], writes=["TS%d" % (c % 2)])
                    for ti, (c0, n) in enumerate(tiles):
                        P.op("pe", I("matmul", PS[ti][:, 0:n], lhsT=ONESS[:, :], rhs=tb[:, c0:c0 + n], start=(c == 0), stop=(c == 7)),
                             reads=["TB%d" % (c % 2), "ONESS"], writes=["ps%d" % ti], signal=False)
                        P.op("pe", I("matmul", PS[3 + ti][:, 0:n], lhsT=ONESS[:, :], rhs=ts[:, c0:c0 + n], start=(c == 0), stop=(c == 7)),
                             reads=["TS%d" % (c % 2), "ONESS"], writes=["ps%d" % (3 + ti)], signal=(ti == len(tiles) - 1))
                for ti, (c0, n) in enumerate(tiles):
                    ms, rs, lt = MSs[ti], RSs[ti], LT[0]
                    P.op("act", I("activation", out=ms[:, 0:n], in_=PS[ti][:, 0:n], func=AF.Identity), reads=["ps%d" % ti], writes=["MS%d" % ti])
                    P.op("dve", I("tensor_tensor", out=lt[:, 0:n], in0=ms[:, 0:n], in1=ms[:, 0:n], op=ALU.mult), reads=["MS%d" % ti], writes=["LT0"])
                    P.op("dve", I("tensor_tensor", out=lt[:, 0:n], in0=PS[3 + ti][:, 0:n], in1=lt[:, 0:n], op=ALU.subtract), reads=["ps%d" % (3 + ti), "LT0"], writes=["LT0"])
                    P.op("dve", I("tensor_scalar", out=lt[:, 0:n], in0=lt[:, 0:n], scalar1=eps, scalar2=None, op0=ALU.add), reads=["LT0"], writes=["LT0"])
                    P.op("act", I("activation", out=lt[:, 0:n], in_=lt[:, 0:n], func=AF.Ln), reads=["LT0"], writes=["LT0"])
                    P.op("act", I("activation", out=rs[:, 0:n], in_=lt[:, 0:n], func=AF.Exp, scale=-0.5), reads=["LT0"], writes=["RS%d" % ti])
                k = 0
                for c in range(8):
                    for ti, (c0, n) in enumerate(tiles):
                        t = LT[k % 2]
                        tk = "LT%d" % (k % 2)
                        k += 1
                        P.op("dve", I("tensor_tensor", out=t[:, 0:n], in0=buf[:, c, c0:c0 + n], in1=MSs[ti][:, 0:n], op=ALU.subtract), reads=[bufk[c], "MS%d" % ti], writes=[tk])
                        P.op("dve", I("tensor_tensor", out=t[:, 0:n], in0=t[:, 0:n], in1=RSs[ti][:, 0:n], op=ALU.mult), reads=[tk, "RS%d" % ti], writes=[tk])
                        outfn(c, c0, n, t, tk)
                P.barrier()

            def post_ln(ls):
                def outfn(c, c0, n, t, tk):
                    P.op("act", I("activation", out=X[:, c, c0:c0 + n], in_=t[:, 0:n], func=AF.Identity, scale=V_("lng", ls * 8 + c), bias=V_("lnb", ls * 8 + c)),
                         reads=[tk, "VEC"], writes=[Xk[c]])
                layer_norm(X, Xk, "lng", "lnb", ls, LN_EPS / (ALPHA * ALPHA), outfn)

            for l in range(N_LAYERS_RUN):
                ls = 2 * l
                if l < 2:
                    modulate(ls)
                    YCk = ["YC%d" % c for c in range(8)]
                    for b in range(4):
                        wv, wk = w_acquire(("pw1", l, b))
                        wv3 = wv[:, :].rearrange("p (k n) -> p k n", k=8)
                        for cc in range(2):
                            c = 2 * b + cc
                            U, US = UB[c % 2], USB[c % 2]
                            Uk, USk = "U%d" % (c % 2), "US%d" % (c % 2)
                            if pi == 0:
                                P.op("dve", I("memset", U[:, 0:30], 0.0), writes=[Uk])
                            else:
                                P.op("dve", I("tensor_copy", out=U[:, 0:30], in_=TAIL[:, l, c, :]), reads=["TAIL"], writes=[Uk])
                                P.dma("sp", "d_us%d" % (c % 2), I("dma_start", out=US[:, :, 0:30], in_=stc[:, l, c, :, :]), writes=[USk])
                            for ti, (c0, n) in enumerate(tiles):
                                ba, bg = nb(), nb()
                                for kc in range(8):
                                    P.op("pe", I("matmul", PS[ba][:, 0:n], lhsT=wv3[:, kc, cc * 128:(cc + 1) * 128], rhs=H[:, kc, c0:c0 + n], start=(kc == 0), stop=(kc == 7)),
                                         reads=[wk] + Hk, writes=["ps%d" % ba], signal=False)
                                for kc in range(8):
                                    P.op("pe", I("matmul", PS[bg][:, 0:n], lhsT=wv3[:, kc, 256 + cc * 128:256 + (cc + 1) * 128], rhs=H[:, kc, c0:c0 + n], start=(kc == 0), stop=(kc == 7)),
                                         reads=[wk] + Hk, writes=["ps%d" % bg], signal=(kc == 7))
                                sg = SIG[ti % 2]
                                sgk = "SIG%d" % (ti % 2)
                                P.op("act", I("activation", out=sg[:, 0:n], in_=PS[bg][:, 0:n], func=AF.Sigmoid, bias=V_("pw1b", l * 16 + 8 + c)),
                                     reads=["ps%d" % bg, "VEC"], writes=[sgk])
                                if ti < 2:
                                    P.op("dve", I("scalar_tensor_tensor", out=U[:, 30 + c0:30 + c0 + n], in0=PS[ba][:, 0:n], scalar=V_("pw1b", l * 16 + c), in1=sg[:, 0:n], op0=ALU.add, op1=ALU.mult),
                                         reads=["ps%d" % ba, sgk, "VEC"], writes=[Uk])
                                else:
                                    P.op("dve", I("scalar_tensor_tensor", out=US[:, :, 30:34], in0=PS[ba][:, 0:16].rearrange("p (s t) -> p s t", s=4), scalar=V_("pw1b", l * 16 + c), in1=sg[:, 0:16].rearrange("p (s t) -> p s t", s=4), op0=ALU.add, op1=ALU.mult),
                                         reads=["ps%d" % ba, sgk, "VEC"], writes=[USk])
                            dwo = l * 248 + c * 31
                            P.op("dve", I("tensor_scalar", out=YC[:, c, 0:1024], in0=U[:, 0:1024], scalar1=V_("dww", dwo), scalar2=V_("dwb", l * 8 + c), op0=ALU.mult, op1=ALU.add),
                                 reads=[Uk, "VEC"], writes=[YCk[c]])
                            for j in range(1, 31):
                                P.op("dve", I("scalar_tensor_tensor", out=YC[:, c, 0:1024], in0=U[:, j:j + 1024], scalar=V_("dww", dwo + j), in1=YC[:, c, 0:1024], op0=ALU.mult, op1=ALU.add),
                                     reads=[Uk, "VEC", YCk[c]], writes=[YCk[c]])
                            if pi == 1:
                                ycs = YC[:, c, 1024:1040].rearrange("p (s t) -> p s t", s=4)
                                P.op("dve", I("tensor_scalar", out=ycs, in0=US[:, :, 0:4], scalar1=V_("dww", dwo), scalar2=V_("dwb", l * 8 + c), op0=ALU.mult, op1=ALU.add),
                                     reads=[USk, "VEC"], writes=[YCk[c]])
                                for j in range(1, 31):
                                    P.op("dve", I("scalar_tensor_tensor", out=ycs, in0=US[:, :, j:j + 4], scalar=V_("dww", dwo + j), in1=ycs, op0=ALU.mult, op1=ALU.add),
                                         reads=[USk, "VEC", YCk[c]], writes=[YCk[c]])
                            if pi == 0:
                                P.op("act", I("activation", out=TAIL[:, l, c, :], in_=U[:, 1024:1054], func=AF.Identity), reads=[Uk], writes=["TAIL"])
                            else:
                                P.dma("sp", "d_ncp%d" % (c % 2), I("dma_start", out=ncp[:, l, c, :], in_=U[:, 1024:1054]), reads=[Uk], writes=["o_ncp"])
                                P.dma("sp", "d_ncs%d" % (c % 2), I("dma_start", out=ncs[:, l, c, :, :], in_=US[:, :, 4:34]), reads=[USk], writes=["o_ncs"])
                        w_release()

                    def conv_out(c, c0, n, t, tk):
                        P.op("act", I("activation", out=H[:, c, c0:c0 + n], in_=t[:, 0:n], func=AF.Silu, scale=V_("clg", l * 8 + c), bias=V_("clb", l * 8 + c)),
                             reads=[tk, "VEC"], writes=[Hk[c]])
                    layer_norm(YC, YCk, "clg", "clb", l, LN_EPS, conv_out)
                    dense_to_resid(lambda ob: ("pw2", l, ob), ls, H, Hk)
                else:
                    j = l - 2
                    P.barrier()
                    if l == 2:
                        for c in range(8):
                            P.op("act", I("activation", out=H[:, c, 0:NC], in_=X[:, c, 0:NC], func=AF.Identity), reads=[Xk[c]], writes=[Hk[c]])
                        toks = [(tk * 128, 128, tk) for tk in range(8)] + ([(1024, 16, 8)] if pi == 1 else [])
                        for b in range(4):
                            wv, wk = w_acquire(("wkv", b))
                            wv3 = wv[:, :].rearrange("p (k n) -> p k n", k=8)
                            if b < 2:
                                for o4 in range(4):
                                    hp = b * 4 + o4
                                    for ti, (c0, n) in enumerate(tiles):
                                        bk = nb()
                                        for kc in range(8):
                                            P.op("pe", I("matmul", PS[bk][:, 0:n], lhsT=wv3[:, kc, o4 * 128:(o4 + 1) * 128], rhs=H[:, kc, c0:c0 + n], start=(kc == 0), stop=(kc == 7)),
                                                 reads=[wk] + Hk, writes=["ps%d" % bk], signal=(kc == 7))
                                        copy_op(evac_engine(), KT[:, hp, c0:c0 + n], PS[bk][:, 0:n], ["ps%d" % bk], ["KT%d_%d_%d" % (pi, hp, ti)])
                            for (t0, tn, tk) in toks:
                                bk = nb()
                                for kc in range(8):
                                    P.op("pe", I("matmul", PS[bk][0:tn, 0:512], lhsT=H[:, kc, t0:t0 + tn], rhs=wv3[:, kc, :], start=(kc == 0), stop=(kc == 7)),
                                         reads=[wk] + Hk, writes=["ps%d" % bk], signal=(kc == 7))
                                sgi = bank_rot["i"] % 2
                                sg = STG[sgi]
                                P.op("act", I("activation", out=sg[0:tn, :], in_=PS[bk][0:tn, :], func=AF.Identity), reads=["ps%d" % bk], writes=["STG%d" % sgi])
                                if tk < 8:
                                    dst = (nkp if b < 2 else nvp)[tbase + t0:tbase + t0 + 128, (b % 2) * 512:(b % 2) * 512 + 512]
                                else:
                                    dst = (nks if b < 2 else nvs)[0:16, (b % 2) * 512:(b % 2) * 512 + 512]
                                P.dma("sp", "d_stg%d" % sgi, I("dma_start", out=dst, in_=sg[0:tn, :]), reads=["STG%d" % sgi], writes=["o_kv"])
                                if b >= 2:
                                    if tk < 8:
                                        vd = VV[:, tk, (b - 2) * 512:(b - 2) * 512 + 512]
                                        vk = "V%d_%d" % (pi, tk)
                                    else:
                                        vd = VN[0:16, (b - 2) * 512:(b - 2) * 512 + 512]
                                        vk = "VN"
                                    P.op("dve", I("tensor_copy", out=vd, in_=sg[0:tn, :]), reads=["STG%d" % sgi], writes=[vk + "_%d" % b])
                            w_release()
                    modulate(ls)
                    for ob in range(2):
                        wv, wk = w_acquire(("wq", j, ob))
                        wv3 = wv[:, :].rearrange("p (k n) -> p k n", k=8)
                        for o4 in range(4):
                            hp = ob * 4 + o4
                            for (c0, n) in tiles:
                                bk = nb()
                                for kc in range(8):
                                    P.op("pe", I("matmul", PS[bk][:, 0:n], lhsT=wv3[:, kc, o4 * 128:(o4 + 1) * 128], rhs=H[:, kc, c0:c0 + n], start=(kc == 0), stop=(kc == 7)),
                                         reads=[wk] + Hk, writes=["ps%d" % bk], signal=(kc == 7))
                                copy_op(evac_engine(), Q[:, hp, c0:c0 + n], PS[bk][:, 0:n], ["ps%d" % bk], ["Q%d" % hp])
                        w_release()
                    P.barrier()

                    def kt_ap(hp, kc, pb):
                        if kc < 8:
                            return KTA[pb:pb + 64, hp, kc * 128:(kc + 1) * 128], "KT0_%d_%d" % (hp, kc // 4)
                        return KTB[pb:pb + 64, hp, (kc - 8) * 128:(kc - 7) * 128], "KT1_%d_%d" % (hp, (kc - 8) // 4)

                    def v_ap(kc, h):
                        if kc < 8:
                            return VA[:, kc, h * 64:(h + 1) * 64], ["V0_%d_%d" % (kc, 2 + h // 8)]
                        return VB[:, kc - 8, h * 64:(h + 1) * 64], ["V1_%d_%d" % (kc - 8, 2 + h // 8)]

                    for hp in (range(8) if do_prompt_attn else []):
                        qn = QN[hp % 2]
                        qnk = "QN%d" % (hp % 2)
                        P.op("dve", I("tensor_scalar", out=qn[:, 0:1024], in0=Q[:, hp, 0:1024], scalar1=-0.125, scalar2=None, op0=ALU.mult), reads=["Q%d" % hp], writes=[qnk])
                        for qi in range(2):
                            gq = 2 * pi + qi
                            qc = qi * 512
                            nch = 4 * gq + 4
                            kcs = list(reversed(range(nch)))

                            def st_Z(sx, i):
                                kc = kcs[i]
                                r = kc - 4 * gq
                                band = r >= 0
                                pb = sx * 64
                                cl = 128 * r if band else 0
                                kap, kk = kt_ap(hp, kc, pb)
                                P.op("pe", I("matmul", PS[sx][:, cl:512], lhsT=kap, rhs=Q[pb:pb + 64, hp, qc + cl:qc + 512], start=True, stop=(not band)),
                                     reads=[kk, "Q%d" % hp], writes=["ps%d" % sx], signal=(not band))
                                if band:
                                    P.op("pe", I("matmul", PS[sx][:, cl:cl + 128], lhsT=IDENT[:, :], rhs=MASK[:, 0, 0:128], start=False, stop=True),
                                         reads=["IDENT", "MASK"], writes=["ps%d" % sx])

                            def st_E(sx, i):
                                h = 2 * hp + sx
                                cl = max(0, 128 * (kcs[i] - 4 * gq))
                                bias_ap = BIAS[:, j * 16 + h:j * 16 + h + 1]
                                sp, spk = SPT[2 * sx + i % 2], "SP%d" % (2 * sx + i % 2)
                                P.op("act", I("activation", out=ET[sx][:, cl:512], in_=PS[sx][:, cl:512], func=AF.Exp, bias=bias_ap, scale=0.125),
                                     reads=["ps%d" % sx, "BIAS"], writes=["ET%d" % sx])
                                P.op("act", I("activation", out=sp[:, cl:512], in_=ET[sx][:, cl:512], func=AF.Ln, bias=1.0, scale=1.0),
                                     reads=["ET%d" % sx], writes=[spk])

                            def st_X(sx, i):
                                kc = kcs[i]
                                r = kc - 4 * gq
                                band = r >= 0
                                pb = sx * 64
                                xb = 2 + sx
                                cl = 128 * r if band else 0
                                kap, kk = kt_ap(hp, kc, pb)
                                sp, spk = SPT[2 * sx + i % 2], "SP%d" % (2 * sx + i % 2)
                                P.op("pe", I("matmul", PS[xb][:, cl:512], lhsT=TRI[:, :], rhs=sp[:, cl:512], start=True, stop=False),
                                     reads=["TRI", spk], writes=["ps%d" % xb], signal=False)
                                if i > 0:
                                    P.op("pe", I("matmul", PS[xb][:, cl:512], lhsT=ONES[:, :], rhs=STs[sx][:, cl:512], start=False, stop=False),
                                         reads=["ONES", "ST%d" % sx], writes=["ps%d" % xb], signal=False)
                                P.op("pe", I("matmul", PS[xb][:, cl:512], lhsT=kap, rhs=qn[pb:pb + 64, qc + cl:qc + 512], start=False, stop=(not band)),
                                     reads=[kk, qnk], writes=["ps%d" % xb], signal=(not band))
                                if band:
                                    P.op("pe", I("matmul", PS[xb][:, cl:cl + 128], lhsT=IDENTN[:, :], rhs=MASK[:, 0, 0:128], start=False, stop=True),
                                         reads=["IDENTN", "MASK"], writes=["ps%d" % xb])

                            def st_W(sx, i):
                                h = 2 * hp + sx
                                bias_ap = BIAS[:, j * 16 + h:j * 16 + h + 1]
                                xb = 2 + sx
                                cl = max(0, 128 * (kcs[i] - 4 * gq))
                                wt, wtk = WT[2 * sx + i % 2], "WT%d" % (2 * sx + i % 2)
                                P.op("act", I("activation", out=wt[:, cl:512], in_=PS[xb][:, cl:512], func=AF.Exp, bias=bias_ap, scale=-1.0),
                                     reads=["ps%d" % xb, "BIAS"], writes=[wtk])

                            def st_S(sx, i):
                                sp, spk = SPT[2 * sx + i % 2], "SP%d" % (2 * sx + i % 2)
                                cl = max(0, 128 * (kcs[i] - 4 * gq))
                                if i < nch - 1:
                                    if i == 0:
                                        if cl > 0:
                                            P.op("dve", I("memset", STs[sx][:, 0:cl], 0.0), writes=["ST%d" % sx])
                                        P.op("dve", I("tensor_copy", out=STs[sx][:, cl:512], in_=sp[:, cl:512]), reads=[spk], writes=["ST%d" % sx])
                                    else:
                                        P.op("dve", I("tensor_tensor", out=STs[sx][:, cl:512], in0=STs[sx][:, cl:512], in1=sp[:, cl:512], op=ALU.add), reads=[spk, "ST%d" % sx], writes=["ST%d" % sx])

                            def st_P(sx, i):
                                kc = kcs[i]
                                h = 2 * hp + sx
                                pb = sx * 64
                                ob_ = 4 + sx + 2 * (qi % 2)
                                wt, wtk = WT[2 * sx + i % 2], "WT%d" % (2 * sx + i % 2)
                                vap, vk = v_ap(kc, h)
                                cl = max(0, 128 * (kc - 4 * gq))
                                P.op("pe", I("matmul", PS[ob_][pb:pb + 64, cl:512], lhsT=vap, rhs=wt[:, cl:512], start=(i == 0), stop=(i == nch - 1), skip_group_check=True),
                                     reads=vk + [wtk], writes=["ps%d" % ob_])

                            for sx in range(2):
                                st_Z(sx, 0)
                            for sx in range(2):
                                st_E(sx, 0)
                            for i in range(nch):
                                for sx in range(2):
                                    st_X(sx, i)
                                if i + 1 < nch:
                                    for sx in range(2):
                                        st_Z(sx, i + 1)
                                for sx in range(2):
                                    st_W(sx, i)
                                for sx in range(2):
                                    st_S(sx, i)
                                for sx in range(2):
                                    st_P(sx, i)
                                if i + 1 < nch:
                                    for sx in range(2):
                                        st_E(sx, i + 1)
                            for sx in range(2):
                                pb = sx * 64
                                ob_ = 4 + sx + 2 * (qi % 2)
                                P.op("dve", I("tensor_copy", out=H[pb:pb + 64, hp, qc:qc + 512], in_=PS[ob_][pb:pb + 64, :]),
                                     reads=["ps%d" % ob_], writes=[Hk[hp]])
                    if pi == 1 and do_sample_attn:
                        P.barrier()
                        for pg in range(4):
                            for qq in range(4):
                                P.op("dve", I("tensor_copy", out=BM[:, j, pg, :].rearrange("p (h q) -> p h q", q=4)[:, :, qq], in_=BIAS[:, j * 16:(j + 1) * 16]), reads=["BIAS"], writes=["BM"])
                        gcount = 0
                        for i in range(4):
                            scol = 1024 + 4 * i
                            P.op("dve", I("tensor_copy", out=QB[0:64, :, 0:4], in_=Q[0:64, :, scol:scol + 4]), reads=["Q%d" % hp for hp in range(8)], writes=["QB"])
                            P.op("dve", I("tensor_copy", out=QB[64:128, :, 4:8], in_=Q[64:128, :, scol:scol + 4]), reads=["Q%d" % hp for hp in range(8)], writes=["QB"])
                            zb = 0
                            for hp in range(8):
                                P.op("pe", I("matmul", PS[0][0:16, hp * 8:hp * 8 + 8], lhsT=KTB[:, hp, 1024:1040], rhs=QB[:, hp, :], start=True, stop=True),
                                     reads=["KT1_%d_2" % hp, "QB"], writes=["ps0"], signal=(hp == 7))
                            P.op("dve", I("scalar_tensor_tensor", out=ZBt[0:16, 0:64], in0=PS[0][0:16, 0:64], scalar=0.125, in1=BM[0:16, j, 0, :], op0=ALU.mult, op1=ALU.add), reads=["ps0", "BM"], writes=["ZBt"])
                            P.op("dve", I("tensor_tensor", out=ZBt[0:16, 0:64], in0=ZBt[0:16, 0:64], in1=MASKN[:, i, :], op=ALU.add), reads=["ZBt", "MASKN"], writes=["ZBt"])
                            P.op("act", I("activation", out=ETs[0:16, 0:64], in_=ZBt[0:16, 0:64], func=AF.Exp), reads=["ZBt"], writes=["ETs"])
                            P.op("act", I("activation", out=SPb[0:16, 0:64], in_=ETs[0:16, 0:64], func=AF.Ln, bias=1.0, scale=1.0), reads=["ETs"], writes=["SPb"])
                            P.op("pe", I("matmul", PS[2][0:16, 0:64], lhsT=TRI[0:16, 0:16], rhs=SPb[0:16, 0:64], start=True, stop=True), reads=["TRI", "SPb"], writes=["ps2"])
                            P.op("dve", I("tensor_tensor", out=TMt[0:16, 0:64], in0=ZBt[0:16, 0:64], in1=PS[2][0:16, 0:64], op=ALU.subtract), reads=["ZBt", "ps2"], writes=["TMt"])
                            P.op("act", I("activation", out=Wb[0:16, 0:64], in_=TMt[0:16, 0:64], func=AF.Exp), reads=["TMt"], writes=["Wb"])
                            for hp in range(8):
                                P.op("pe", I("matmul", PS[4][:, hp * 8:hp * 8 + 8], lhsT=VN[0:16, hp * 128:(hp + 1) * 128], rhs=Wb[0:16, hp * 8:hp * 8 + 8], start=True, stop=True),
                                     reads=["VN_2", "VN_3", "Wb"], writes=["ps4"], signal=(hp == 7))
                            P.op("dve", I("tensor_copy", out=OACC[:, :], in_=PS[4][:, 0:64]), reads=["ps4"], writes=["OACC"])
                            P.op("dve", I("memset", SC[:, :], 0.0), writes=["SC"])
                            P.op("dve", I("tensor_copy", out=SC[0:16, :], in_=SPb[0:16, 0:64]), reads=["SPb"], writes=["SC"])
                            TP = 4
                            TW = TP * 64
                            for t in reversed(range(64 // TP)):
                                zbk = t % 2
                                xbk = 2 + t % 2
                                for pg in reversed(range(TP)):
                                    page = TP * t + pg
                                    ks = gcount % 3
                                    kts = 0
                                    gcount += 1
                                    kb = KBs[ks]
                                    KTp = KTps[kts]
                                    vs = (t % 2) * TP + pg
                                    col = i * 64 + page
                                    P.dma("pool", "d_kb%d" % ks, I("indirect_dma_start", out=kb[:, :], out_offset=None, in_=ck, in_offset=bass.IndirectOffsetOnAxis(ap=IDX[:, col:col + 1], axis=0)),
                                          reads=["IDX"], writes=["KB%d" % ks])
                                    P.dma("pool", "d_vb%d" % vs, I("indirect_dma_start", out=VBs[vs][:, :], out_offset=None, in_=cv, in_offset=bass.IndirectOffsetOnAxis(ap=IDX[:, col:col + 1], axis=0)),
                                          reads=["IDX"], writes=["VB%d" % vs])
                                    for hp in range(8):
                                        P.op("pe", I("transpose", out=PST[:, hp * 128:(hp + 1) * 128], in_=kb[:, hp * 128:(hp + 1) * 128], identity=IDENT[:, :]),
                                             reads=["KB%d" % ks, "IDENT"], writes=["ps7"], signal=(hp == 7))
                                    P.op("act", I("activation", out=KTp[:, :], in_=PST[:, :], func=AF.Identity), reads=["ps7"], writes=["KTp%d" % kts])
                                    for hp in range(8):
                                        P.op("pe", I("matmul", PS[zbk][:, pg * 64 + hp * 8:pg * 64 + hp * 8 + 8], lhsT=KTp[:, hp * 128:(hp + 1) * 128], rhs=QB[:, hp, :], start=True, stop=True),
                                             reads=["KTp%d" % kts, "QB"], writes=["ps%d" % zbk], signal=(hp == 7))
                                P.op("dve", I("scalar_tensor_tensor", out=ZBt[:, 0:TW], in0=PS[zbk][:, 0:TW], scalar=0.125, in1=BM[:, j, :, :].rearrange("p a b -> p (a b)"), op0=ALU.mult, op1=ALU.add), reads=["ps%d" % zbk, "BM"], writes=["ZBt"])
                                P.op("act", I("activation", out=ETs[:, 0:TW], in_=ZBt[:, 0:TW], func=AF.Exp), reads=["ZBt"], writes=["ETs"])
                                P.op("act", I("activation", out=SPb[:, 0:TW], in_=ETs[:, 0:TW], func=AF.Ln, bias=1.0, scale=1.0), reads=["ETs"], writes=["SPb"])
                                P.op("dve", I("tensor_copy", out=Stl[:, (TP - 1) * 64:TP * 64], in_=SC[:, :]), reads=["SC"], writes=["Stl"])
                                for pg in reversed(range(TP - 1)):
                                    P.op("dve", I("tensor_tensor", out=Stl[:, pg * 64:(pg + 1) * 64], in0=Stl[:, (pg + 1) * 64:(pg + 2) * 64], in1=SPb[:, (pg + 1) * 64:(pg + 2) * 64], op=ALU.add), reads=["Stl", "SPb"], writes=["Stl"])
                                P.op("dve", I("tensor_tensor", out=SC[:, :], in0=Stl[:, 0:64], in1=SPb[:, 0:64], op=ALU.add), reads=["Stl", "SPb"], writes=["SC"])
                                P.op("dve", I("tensor_copy", out=Sb[:, 0:TW], in_=Stl[:, 0:TW]), reads=["Stl"], writes=["Sb"])
                                P.op("pe", I("matmul", PS[xbk][:, 0:TW], lhsT=TRI[:, :], rhs=SPb[:, 0:TW], start=True, stop=False), reads=["TRI", "SPb"], writes=["ps%d" % xbk], signal=False)
                                P.op("pe", I("matmul", PS[xbk][:, 0:TW], lhsT=ONES[:, :], rhs=Sb[:, 0:TW], start=False, stop=True), reads=["ONES", "Sb"], writes=["ps%d" % xbk])
                                P.op("dve", I("tensor_tensor", out=TMt[:, 0:TW], in0=ZBt[:, 0:TW], in1=PS[xbk][:, 0:TW], op=ALU.subtract), reads=["ZBt", "ps%d" % xbk], writes=["TMt"])
                                P.op("act", I("activation", out=Wb[:, 0:TW], in_=TMt[:, 0:TW], func=AF.Exp), reads=["TMt"], writes=["Wb"])
                                for hp in range(8):
                                    for pg in range(TP):
                                        vs = (t % 2) * TP + pg
                                        P.op("pe", I("matmul", PS[4][:, hp * 8:hp * 8 + 8], lhsT=VBs[vs][:, hp * 128:(hp + 1) * 128], rhs=Wb[:, pg * 64 + hp * 8:pg * 64 + hp * 8 + 8], start=(pg == 0), stop=(pg == TP - 1)),
                                             reads=["VB%d" % vs, "Wb"], writes=["ps4"], signal=(hp == 7 and pg == TP - 1))
                                P.op("dve", I("tensor_tensor", out=OACC[:, :], in0=OACC[:, :], in1=PS[4][:, 0:64], op=ALU.add), reads=["ps4", "OACC"], writes=["OACC"])
                            oa = OACC[:, :].rearrange("p (h c) -> p h c", c=8)
                            P.op("dve", I("tensor_copy", out=H[0:64, :, scol:scol + 4], in_=oa[0:64, :, 0:4]), reads=["OACC"], writes=Hk)
                            P.op("dve", I("tensor_copy", out=H[64:128, :, scol:scol + 4], in_=oa[64:128, :, 4:8]), reads=["OACC"], writes=Hk)
                    P.barrier()
                    dense_to_resid(lambda ob: ("wo", j, ob), ls, H, Hk)
                post_ln(ls)

                ls = 2 * l + 1
                modulate(ls)
                k = 0
                for jb in range(8):
                    wv, wk = w_acquire(("w1", l, jb))
                    wv3 = wv[:, :].rearrange("p (k n) -> p k n", k=8)
                    hid, hidk = HID[jb % 2], "HID%d" % (jb % 2)
                    for hc in range(4):
                        for (c0, n) in tiles:
                            bk = nb()
                            for kc in range(8):
                                P.op("pe", I("matmul", PS[bk][:, 0:n], lhsT=wv3[:, kc, hc * 128:(hc + 1) * 128], rhs=H[:, kc, c0:c0 + n], start=(kc == 0), stop=(kc == 7)),
                                     reads=[wk] + Hk, writes=["ps%d" % bk], signal=(kc == 7))
                            rl, rlk = RL[k % 2], "RL%d" % (k % 2)
                            k += 1
                            P.op("act", I("activation", out=rl[:, 0:n], in_=PS[bk][:, 0:n], func=AF.Relu), reads=["ps%d" % bk], writes=[rlk])
                            P.op("dve", I("tensor_tensor", out=hid[:, hc, c0:c0 + n], in0=rl[:, 0:n], in1=rl[:, 0:n], op=ALU.mult), reads=[rlk], writes=[hidk])
                    w_release()
                    ag = ada_gap(pi, l, 2 * jb) if N_LAYERS_RUN == 4 else None
                    if ag:
                        ada_block(ag[0], ag[1])
                    wv, wk = w_acquire(("w2", l, jb))
                    wv4 = wv[:, :].rearrange("p (k n) -> p k n", k=4)
                    for oc in range(8):
                        for (c0, n) in tiles:
                            bk = nb()
                            for hc in range(4):
                                P.op("pe", I("matmul", PS[bk][:, 0:n], lhsT=wv4[:, hc, oc * 128:(oc + 1) * 128], rhs=hid[:, hc, c0:c0 + n], start=(hc == 0), stop=(hc == 3)),
                                     reads=[wk, hidk], writes=["ps%d" % bk], signal=(hc == 3))
                            resid_add(ls, oc, c0, n, bk)
                    w_release()
                    ag = ada_gap(pi, l, 2 * jb + 1) if N_LAYERS_RUN == 4 else None
                    if ag:
                        ada_block(ag[0], ag[1])
                post_ln(ls)

            P.dma("sp", "d_y", I("dma_start", out=yp[:, :, tbase:tbase + 1024], in_=X[:, :, 0:1024]), reads=Xk, writes=["o_y"])
            if pi == 1:
                P.dma("sp", "d_y", I("dma_start", out=ys, in_=X[:, :, 1024:1040]), reads=Xk, writes=["o_ys"])
        assert wst["use"] == len(plan), (wst["use"], len(plan))
        P.emit()
    return nc


def _fm(v):
    v = np.asarray(v, np.float32)
    lead = v.shape[:-1]
    w = v.reshape(lead + (8, 128))
    return np.ascontiguousarray(np.moveaxis(w, -1, 0))


_NC_CACHE = {}


def _prep(x_prompt, x_sample, c_prompt, c_sample, state_conv, cache_k, cache_v, page_table,
          ada_w, ada_b, ln_g, ln_b, mlp_w1, mlp_w2, conv_pw1_w, conv_pw1_b, conv_dw_w, conv_dw_b,
          conv_ln_g, conv_ln_b, conv_pw2_w, w_kv, attn_wq, attn_wo, attn_bias, cores=range(8)):
    f = np.float32
    npool = cache_k.shape[0]
    ck = np.ascontiguousarray(np.asarray(cache_k, f).reshape(npool * 128, 1024))
    cv = np.ascontiguousarray(np.asarray(cache_v, f).reshape(npool * 128, 1024))
    adaw = np.ascontiguousarray(np.asarray(ada_w, f).reshape(8, 1024, 3072))
    shared = dict(
        ck=ck, cv=cv, adaw=adaw,
        w1=np.ascontiguousarray(np.asarray(mlp_w1, f)), w2=np.ascontiguousarray(np.asarray(mlp_w2, f)),
        pw1=np.ascontiguousarray(np.asarray(conv_pw1_w, f)), pw2=np.ascontiguousarray(np.asarray(conv_pw2_w, f)),
        wkv=np.ascontiguousarray(np.asarray(w_kv, f)), wq=np.ascontiguousarray(np.asarray(attn_wq, f)),
        wo=np.ascontiguousarray(np.asarray(attn_wo, f)),
        abias=np.ascontiguousarray(np.asarray(attn_bias, f).reshape(1, 32)),
    )
    vec_common = np.zeros((128, NVEC), f)
    ab = np.asarray(ada_b, f).reshape(8, 24, 128)
    vec_common[:, VOFF["adab"]:VOFF["adab"] + 192] = ab.transpose(2, 0, 1).reshape(128, 192)
    vec_common[:, VOFF["lng"]:VOFF["lng"] + 64] = _fm(np.asarray(ln_g, f).reshape(8, 1024)).reshape(128, 64)
    vec_common[:, VOFF["lnb"]:VOFF["lnb"] + 64] = _fm(np.asarray(ln_b, f).reshape(8, 1024)).reshape(128, 64)
    pb = np.asarray(conv_pw1_b, f).reshape(2, 16, 128)
    vec_common[:, VOFF["pw1b"]:VOFF["pw1b"] + 32] = pb.transpose(2, 0, 1).reshape(128, 32)
    dw = np.asarray(conv_dw_w, f).reshape(2, 31, 8, 128)
    vec_common[:, VOFF["dww"]:VOFF["dww"] + 496] = dw.transpose(3, 0, 2, 1).reshape(128, 496)
    vec_common[:, VOFF["dwb"]:VOFF["dwb"] + 16] = _fm(np.asarray(conv_dw_b, f)).reshape(128, 16)
    vec_common[:, VOFF["clg"]:VOFF["clg"] + 16] = _fm(np.asarray(conv_ln_g, f)).reshape(128, 16)
    vec_common[:, VOFF["clb"]:VOFF["clb"] + 16] = _fm(np.asarray(conv_ln_b, f)).reshape(128, 16)
    xpr = np.asarray(x_prompt, f)
    xsa = np.asarray(x_sample, f)
    stv = np.asarray(state_conv, f)
    in_maps = []
    for c in cores:
        m = dict(shared)
        m["xp"] = np.ascontiguousarray(xpr[c].T.reshape(8, 128, 2048).transpose(1, 0, 2))
        m["xs"] = np.ascontiguousarray(xsa[4 * c:4 * c + 4].reshape(16, 1024).T.reshape(8, 128, 16).transpose(1, 0, 2))
        v = vec_common.copy()
        cc = np.concatenate([np.asarray(c_prompt, f)[c:c + 1], np.asarray(c_sample, f)[4 * c:4 * c + 4]], 0)
        v[:, VOFF["ct"]:VOFF["ct"] + 40] = cc.reshape(5, 8, 128).transpose(2, 1, 0).reshape(128, 40)
        m["vec"] = v
        s = stv[:, 4 * c:4 * c + 4].reshape(2, 4, 30, 8, 128)
        m["stc"] = np.ascontiguousarray(s.transpose(4, 0, 3, 1, 2))
        m["ptab"] = np.ascontiguousarray(np.asarray(page_table, np.int32)[4 * c:4 * c + 4])
        in_maps.append(m)
    return in_maps


def _assemble(res):
    f = np.float32

    def unfm(a):
        a = np.asarray(a)
        return np.ascontiguousarray(a.transpose(2, 1, 0).reshape(a.shape[2], 1024))

    y_prompt = np.stack([unfm(r["yp"]) for r in res], 0)
    y_sample = np.concatenate([unfm(r["ys"]).reshape(4, 4, 1024) for r in res], 0)
    ncp_ = np.stack([np.asarray(r["ncp"]).transpose(1, 3, 2, 0).reshape(2, 30, 1024) for r in res], 1)
    ncs_ = np.concatenate([np.asarray(r["ncs"]).transpose(1, 3, 4, 2, 0).reshape(2, 4, 30, 1024) for r in res], 1)
    nk_p = np.stack([np.asarray(r["nkp"]).reshape(2048, 16, 64) for r in res], 0)
    nv_p = np.stack([np.asarray(r["nvp"]).reshape(2048, 16, 64) for r in res], 0)
    nk_s = np.concatenate([np.asarray(r["nks"]).reshape(4, 4, 16, 64) for r in res], 0)
    nv_s = np.concatenate([np.asarray(r["nvs"]).reshape(4, 4, 16, 64) for r in res], 0)
    return (y_prompt.astype(f), y_sample.astype(f), np.ascontiguousarray(ncp_).astype(f),
            np.ascontiguousarray(ncs_).astype(f), nk_p.astype(f), nv_p.astype(f), nk_s.astype(f), nv_s.astype(f))


def kernel(**inputs):
    if "nc" not in _NC_CACHE:
        _NC_CACHE["nc"] = build_program()
    nc = _NC_CACHE["nc"]
    in_maps = _prep(**inputs)
    res = run_bass_kernel_spmd(nc, in_maps, core_ids=list(range(8))).results
    return _assemble(res)
```

```python
import contextlib
import numpy as np
import concourse.bass as bass
import concourse.mybir as mybir
from concourse.bass_utils import run_bass_kernel_spmd

F32 = mybir.dt.float32
BF16 = mybir.dt.bfloat16
I32 = mybir.dt.int32
AF = mybir.ActivationFunctionType
ALU = mybir.AluOpType

ALPHA = 8.0 ** 0.25
LN_EPS = 1e-5
NEG = -240000.0
ENGS = ("pe", "act", "dve", "pool", "sp")
import os as _os
SAME_ENGINE_SYNC = bool(int(_os.environ.get("KSES", "0")))


def I(method, *args, **kwargs):
    return (method, args, kwargs)


class Prog:
    def __init__(self, nc):
        self.nc = nc
        self.ops = {e: [] for e in ENGS}
        self.nsig = {e: 0 for e in ENGS}
        self.pending = {e: False for e in ENGS}
        self.last_w = {}
        self.readers = {}
        self.dma_cnt = {}
        self.waited = {e: {} for e in ENGS}
        self.extra = {e: [] for e in ENGS}
        self.small_ev = {}

    def _deps(self, eng, reads, writes, skip_sem=None, small=True):
        deps = {}
        need_same = [small]

        def add(ev):
            if ev is None:
                return
            s, v = ev
            if s == eng and self.small_ev.get((s, v), True):
                need_same[0] = True
            if deps.get(s, 0) < v:
                deps[s] = v

        for k in reads:
            add(self.last_w.get(k))
        for k in writes:
            add(self.last_w.get(k))
            for s, v in self.readers.get(k, {}).items():
                add((s, v))
        for ev in self.extra[eng]:
            add(ev)
        self.extra[eng] = []
        out = []
        for s, v in deps.items():
            if s == skip_sem:
                continue
            if s == eng:
                if eng == "pe":
                    continue
                if not SAME_ENGINE_SYNC and not need_same[0]:
                    continue
                if v > self.nsig[eng]:
                    continue
            if self.waited[eng].get(s, 0) >= v:
                continue
            self.waited[eng][s] = v
            out.append((s, v))
        return out

    def _register(self, ev, reads, writes):
        for k in reads:
            r = self.readers.setdefault(k, {})
            if r.get(ev[0], 0) < ev[1]:
                r[ev[0]] = ev[1]
        for k in writes:
            self.last_w[k] = ev
            self.readers[k] = {}

    def op(self, eng, fn, reads=(), writes=(), signal=True):
        small = True
        if eng != "pe":
            o = fn[2].get("out", fn[1][0] if fn[1] else None)
            try:
                small = o is None or o.free_size() < 256
            except Exception:
                small = True
        waits = self._deps(eng, reads, writes, small=small)
        ev = (eng, self.nsig[eng] + 1)
        self.small_ev[ev] = small
        inc = None
        if signal:
            self.nsig[eng] += 1
            inc = (eng, 1)
            self.pending[eng] = False
        else:
            self.pending[eng] = True
        self._register(ev, reads, writes)
        self.ops[eng].append((waits, fn, inc))

    def raw(self, eng, fn):
        self.ops[eng].append(([], fn, None))

    def dma(self, q, sem, fn, reads=(), writes=()):
        waits = self._deps(q, reads, writes, skip_sem=sem)
        self.dma_cnt[sem] = self.dma_cnt.get(sem, 0) + 1
        ev = (sem, 16 * self.dma_cnt[sem])
        self._register(ev, reads, writes)
        self.ops[q].append((waits, fn, (sem, 16)))

    def barrier(self):
        evs = [(e, self.nsig[e]) for e in ENGS if self.nsig[e] > 0]
        evs += [(s, 16 * c) for s, c in self.dma_cnt.items() if not s.startswith("d_w")]
        for e in ENGS:
            assert not self.pending[e]
            self.extra[e] = list(evs)

    def emit(self):
        nc = self.nc
        for e in ENGS:
            assert not self.pending[e], e
        names = list(ENGS) + sorted(self.dma_cnt.keys())
        fin = [(e, self.nsig[e]) for e in ENGS if e != "sp" and self.nsig[e] > 0]
        fin += [(s, 16 * c) for s, c in self.dma_cnt.items()]
        with contextlib.ExitStack() as st:
            sems = {n: st.enter_context(nc.semaphore("s_" + n)) for n in names}
            block = st.enter_context(nc.Block())

            def run(name, eng):
                for waits, fn, inc in self.ops[name]:
                    for s, v in waits:
                        eng.wait_ge(sems[s], v)
                    ins = getattr(eng, fn[0])(*fn[1], **fn[2])
                    if inc is not None:
                        ins.then_inc(sems[inc[0]], inc[1])

            @block.tensor
            def _(e):
                run("pe", e)

            @block.scalar
            def _(e):
                run("act", e)

            @block.vector
            def _(e):
                run("dve", e)

            @block.gpsimd
            def _(e):
                run("pool", e)

            @block.sync
            def _(e):
                run("sp", e)
                for s, v in fin:
                    e.wait_ge(sems[s], v)


VOFF = {}
_o = 0
for _n, _sz in (("adab", 192), ("lng", 64), ("lnb", 64), ("pw1b", 32), ("dww", 496), ("dwb", 16),
                ("clg", 16), ("clb", 16), ("ct", 40)):
    VOFF[_n] = _o
    _o += _sz
NVEC = _o


def build_program(npool=2560, N_LAYERS_RUN=4, passes=(0, 1), do_sample_attn=True, do_prompt_attn=True):
    nc = bass.Bass("TRN2", target_bir_lowering=False)

    def din(name, shape, dt=F32):
        return nc.dram_tensor(name, list(shape), dt, kind="ExternalInput").ap()

    def dout(name, shape, dt=F32):
        return nc.dram_tensor(name, list(shape), dt, kind="ExternalOutput").ap()

    xp = din("xp", [128, 8, 2048])
    xs = din("xs", [128, 8, 16])
    vec = din("vec", [128, NVEC])
    stc = din("stc", [128, 2, 8, 4, 30])
    ck = din("ck", [npool * 128, 1024])
    cv = din("cv", [npool * 128, 1024])
    ptab = din("ptab", [4, 64], I32)
    abias = din("abias", [1, 32])
    adaw = din("adaw", [8, 1024, 3072])
    w1 = din("w1", [4, 1024, 4096])
    w2 = din("w2", [4, 4096, 1024])
    pw1 = din("pw1", [2, 1024, 2048])
    pw2 = din("pw2", [2, 1024, 1024])
    wkv = din("wkv", [1024, 2048])
    wq = din("wq", [2, 1024, 1024])
    wo = din("wo", [2, 1024, 1024])

    yp = dout("yp", [128, 8, 2048])
    ys = dout("ys", [128, 8, 16])
    ncp = dout("ncp", [128, 2, 8, 30])
    ncs = dout("ncs", [128, 2, 8, 4, 30])
    nkp = dout("nkp", [2048, 1024])
    nvp = dout("nvp", [2048, 1024])
    nks = dout("nks", [16, 1024])
    nvs = dout("nvs", [16, 1024])

    P = Prog(nc)
    with contextlib.ExitStack() as st:
        def sb(name, shape, dt):
            return st.enter_context(nc.sbuf_tensor(name, list(shape), dt))

        X = sb("X", [128, 8, 1040], F32)
        H = sb("H", [128, 8, 1040], BF16)
        KTA = sb("KTA", [128, 8, 1024], BF16)
        VA = sb("VA", [128, 8, 1024], BF16)
        RB = sb("RB", [128, 8768], F32)
        WB = [sb("WB0", [128, 4096], BF16), sb("WB1", [128, 4096], BF16)]
        R1 = sb("R1", [128, 11900], F32)
        MOD = sb("MOD", [128, 8, 24, 5], F32)
        VEC = sb("VEC", [128, NVEC], F32)
        IDENT = sb("IDENT", [128, 128], BF16)
        IDENTN = sb("IDENTN", [128, 128], BF16)
        TRI = sb("TRI", [128, 128], BF16)
        ONES = sb("ONES", [128, 128], BF16)
        ONESS = sb("ONESS", [128, 128], BF16)
        CFILL = sb("CFILL", [128, 512], BF16)
        MASK = sb("MASK", [128, 4, 512], BF16)
        BIAS = sb("BIAS", [128, 32], F32)
        TAIL = sb("TAIL", [128, 2, 8, 30], F32)
        SCT = sb("SCT", [128, 8, 5], BF16)
        STG = [sb("STG0", [128, 512], F32), sb("STG1", [128, 512], F32)]
        MASKN = sb("MASKN", [16, 4, 64], F32)
        ZF = sb("ZF", [16, 64], F32)
        BM = sb("BM", [128, 2, 4, 64], F32)
        QB = sb("QB", [128, 8, 8], BF16)
        PTB = sb("PTB", [128, 256], I32)
        PTF = sb("PTF", [128, 256], F32)
        IOP = sb("IOP", [128, 1], F32)
        IDX = sb("IDX", [128, 256], I32)
        G16 = sb("G16", [128, 8, 8, 16], F32)
        TMP16 = sb("TMP16", [128, 2, 16], F32)
        PS = [st.enter_context(nc.psum_tensor("PS%d" % i, [128, 512], F32)) for i in range(8)]
        PST = PS[7][:, :].bitcast(BF16)

        KTB = RB[:, 0:4160].bitcast(BF16).rearrange("p (c n) -> p c n", c=8)
        VB = RB[:, 4160:8256].bitcast(BF16).rearrange("p (c n) -> p c n", c=8)
        VN = RB[:, 8256:8768].bitcast(BF16)
        YC = RB[:, 0:8320].rearrange("p (c n) -> p c n", c=8)

        def r1(off, n, dt=F32):
            nf = n if dt == F32 else (n + 1) // 2
            v = R1[:, off:off + nf]
            if dt != F32:
                v = v.bitcast(dt)
            return v, off + nf

        o = 0
        UB = []
        for i in range(2):
            v, o = r1(o, 1054)
            UB.append(v)
        USB = []
        for i in range(2):
            v, o = r1(o, 136)
            USB.append(v.rearrange("p (s n) -> p s n", s=4))
        SIG = []
        for i in range(2):
            v, o = r1(o, 512)
            SIG.append(v)
        o = 0
        HID = []
        for i in range(2):
            v, o = r1(o, 4 * 1040, BF16)
            HID.append(v.rearrange("p (c n) -> p c n", c=4))
        RL = []
        for i in range(2):
            v, o = r1(o, 512)
            RL.append(v)
        o = 0
        TBs, TSs = [], []
        for i in range(2):
            v, o = r1(o, 1040, BF16)
            TBs.append(v)
        for i in range(2):
            v, o = r1(o, 1040, BF16)
            TSs.append(v)
        MSs, RSs = [], []
        for i in range(3):
            v, o = r1(o, 512)
            MSs.append(v)
        for i in range(3):
            v, o = r1(o, 512)
            RSs.append(v)
        LT = []
        for i in range(2):
            v, o = r1(o, 512)
            LT.append(v)
        o = 0
        Q, o = r1(o, 8 * 1040, BF16)
        Q = Q.rearrange("p (c n) -> p c n", c=8)
        o_attn = o
        QN = []
        for i in range(2):
            v, o = r1(o, 1024, BF16)
            QN.append(v)
        ET = []
        for i in range(2):
            v, o = r1(o, 512)
            ET.append(v)
        SPT = []
        for i in range(4):
            v, o = r1(o, 512, BF16)
            SPT.append(v)
        WT = []
        for i in range(4):
            v, o = r1(o, 512, BF16)
            WT.append(v)
        STs = []
        for i in range(2):
            v, o = r1(o, 512, BF16)
            STs.append(v)
        assert o <= 11900, o
        o = o_attn
        KBs = []
        for i in range(3):
            v, o = r1(o, 1024, BF16)
            KBs.append(v)
        VBs = []
        for i in range(8):
            v, o = r1(o, 1024, BF16)
            VBs.append(v)
        KTps = []
        for i in range(1):
            v, o = r1(o, 1024, BF16)
            KTps.append(v)
        ZBt, o = r1(o, 256)
        ETs, o = r1(o, 256)
        TMt, o = r1(o, 256)
        SPb, o = r1(o, 256, BF16)
        Wb, o = r1(o, 256, BF16)
        Stl, o = r1(o, 256)
        Sb, o = r1(o, 256, BF16)
        SC, o = r1(o, 64)
        OACC, o = r1(o, 64)
        assert o <= 11900, o

        plan = []

        def blk1024(wap, col0):
            def f(slot):
                dst = WB[slot][:, :].rearrange("p (k n) -> p k n", k=8)
                src = wap[:, col0:col0 + 512].rearrange("(k p) n -> p k n", p=128)
                return [I("dma_start", out=dst, in_=src)]
            return f

        def blkw2(l, j):
            def f(slot):
                dst = WB[slot][:, :].rearrange("p (k n) -> p k n", k=4)
                src = w2[l][j * 512:(j + 1) * 512, :].rearrange("(k p) n -> p k n", p=128)
                return [I("dma_start", out=dst, in_=src)]
            return f

        def blkpw1(l, b):
            def f(slot):
                dst = WB[slot][:, :].rearrange("p (k n) -> p k n", k=8)
                sa = pw1[l][:, b * 256:(b + 1) * 256].rearrange("(k p) n -> p k n", p=128)
                sg = pw1[l][:, 1024 + b * 256:1024 + (b + 1) * 256].rearrange("(k p) n -> p k n", p=128)
                return [I("dma_start", out=dst[:, :, 0:256], in_=sa),
                        I("dma_start", out=dst[:, :, 256:512], in_=sg)]
            return f

        for ls in range(8):
            for b in range(6):
                plan.append((("ada", ls, b), blk1024(adaw[ls], b * 512)))
        for pi in passes:
            for l in range(N_LAYERS_RUN):
                if l < 2:
                    for b in range(4):
                        plan.append((("pw1", l, b), blkpw1(l, b)))
                    for b in range(2):
                        plan.append((("pw2", l, b), blk1024(pw2[l], b * 512)))
                else:
                    if l == 2:
                        for b in range(4):
                            plan.append((("wkv", b), blk1024(wkv, b * 512)))
                    for b in range(2):
                        plan.append((("wq", l - 2, b), blk1024(wq[l - 2], b * 512)))
                    for b in range(2):
                        plan.append((("wo", l - 2, b), blk1024(wo[l - 2], b * 512)))
                for j in range(8):
                    plan.append((("w1", l, j), blk1024(w1[l], j * 512)))
                    plan.append((("w2", l, j), blkw2(l, j)))

        wst = {"issue": 0, "use": 0}

        def w_issue():
            i = wst["issue"]
            if i >= len(plan):
                return
            slot = i % 2
            for fn in plan[i][1](slot):
                P.dma("pool", "d_w%d" % slot, fn, writes=["W%d" % slot])
            wst["issue"] += 1

        def w_acquire(tag):
            i = wst["use"]
            assert plan[i][0] == tag, (plan[i][0], tag)
            slot = i % 2
            return WB[slot], "W%d" % slot

        def w_release():
            wst["use"] += 1
            w_issue()

        bank_rot = {"i": 0}
        tmp16_rot = {"i": 0}

        def nb(n=6):
            b = bank_rot["i"] % n
            bank_rot["i"] += 1
            return b

        alt = {"i": 0}

        def evac_engine():
            alt["i"] += 1
            return "act" if alt["i"] % 2 else "dve"

        def copy_op(eng, out, in_, reads, writes):
            if eng == "act":
                P.op("act", I("activation", out=out, in_=in_, func=AF.Identity), reads=reads, writes=writes)
            else:
                P.op(eng, I("tensor_copy", out=out, in_=in_), reads=reads, writes=writes)

        def V_(name, i=0, n=1):
            o = VOFF[name] + i
            return VEC[:, o:o + n]

        P.dma("sp", "d_vec", I("dma_start", out=VEC[:, :], in_=vec), writes=["VEC"])
        P.dma("sp", "d_bias", I("dma_start", out=BIAS[:, :], in_=abias[0:1, :].partition_broadcast(128)), writes=["BIAS"])
        P.dma("sp", "d_pt", I("dma_start", out=PTB[:, :], in_=ptab.rearrange("(o s) n -> o (s n)", o=1).partition_broadcast(128)), writes=["PTB"])
        w_issue()
        w_issue()
        P.op("pool", I("memset", ONES[:, :], 1.0), writes=["ONES"])
        P.op("pool", I("memset", ONESS[:, :], 1.0 / 1024.0), writes=["ONESS"])
        P.op("pool", I("memset", CFILL[:, :], 0.0), writes=["CFILL"])
        P.op("pool", I("affine_select", out=TRI[:, :], in_=ONES[:, :], pattern=[[-1, 128]], compare_op=ALU.is_ge, fill=0.0, base=0, channel_multiplier=1), reads=["ONES"], writes=["TRI"])
        P.op("pool", I("affine_select", out=IDENT[:, :], in_=ONES[:, :], pattern=[[-1, 128]], compare_op=ALU.is_equal, fill=0.0, base=0, channel_multiplier=1), reads=["ONES"], writes=["IDENT"])
        P.op("pool", I("tensor_scalar", out=IDENTN[:, :], in0=IDENT[:, :], scalar1=-0.125, scalar2=None, op0=ALU.mult), reads=["IDENT"], writes=["IDENTN"])
        for r in range(4):
            P.op("pool", I("affine_select", out=MASK[:, r, :], in_=CFILL[:, :], pattern=[[1, 512]], compare_op=ALU.is_gt, fill=NEG, base=-128 * r, channel_multiplier=-1), reads=["CFILL"], writes=["MASK"])
        P.op("pool", I("memset", ZF[:, :], 0.0), writes=["ZF"])
        for i in range(4):
            P.op("pool", I("affine_select", out=MASKN[:, i, :], in_=ZF[:, :], pattern=[[0, 64]], compare_op=ALU.is_ge, fill=-30000.0, base=-4 * i, channel_multiplier=1), reads=["ZF"], writes=["MASKN"])
            P.op("pool", I("affine_select", out=MASKN[:, i, :], in_=MASKN[:, i, :], pattern=[[0, 16], [1, 4]], compare_op=ALU.is_gt, fill=-30000.0, base=4 * i, channel_multiplier=-1), reads=["MASKN"], writes=["MASKN"])
        P.op("pool", I("memset", QB[:, :, :], 0.0), writes=["QB"])
        P.op("pool", I("iota", IOP[:, :], pattern=[[0, 1]], base=0, channel_multiplier=1, allow_small_or_imprecise_dtypes=True), writes=["IOP"])
        P.op("dve", I("tensor_copy", out=PTF[:, :], in_=PTB[:, :]), reads=["PTB"], writes=["PTF"])
        P.op("dve", I("tensor_scalar", out=PTF[:, :], in0=PTF[:, :], scalar1=128.0, scalar2=IOP[:, 0:1], op0=ALU.mult, op1=ALU.add), reads=["PTF", "IOP"], writes=["PTF"])
        P.op("dve", I("tensor_copy", out=IDX[:, :], in_=PTF[:, :]), reads=["PTF"], writes=["IDX"])

        P.op("act", I("activation", out=SCT[:, :, :], in_=V_("ct", 0, 40).rearrange("p (k s) -> p k s", k=8), func=AF.Silu), reads=["VEC"], writes=["SCT"])
        for ls in range(8):
            for b in range(6):
                wv, wk = w_acquire(("ada", ls, b))
                wv3 = wv[:, :].rearrange("p (k n) -> p k n", k=8)
                bk = b % 2
                for oc in range(4):
                    for kc in range(8):
                        P.op("pe", I("matmul", PS[bk][:, oc * 5:oc * 5 + 5], lhsT=wv3[:, kc, oc * 128:(oc + 1) * 128], rhs=SCT[:, kc, :], start=(kc == 0), stop=(kc == 7)),
                             reads=[wk, "SCT"], writes=["ps%d" % bk], signal=(oc == 3 and kc == 7))
                for oc in range(4):
                    og = b * 4 + oc
                    P.op("dve", I("tensor_scalar", out=MOD[:, ls, og, :], in0=PS[bk][:, oc * 5:oc * 5 + 5], scalar1=V_("adab", ls * 24 + og), scalar2=None, op0=ALU.add),
                         reads=["ps%d" % bk, "VEC"], writes=["MOD"])
                w_release()
            P.op("dve", I("tensor_scalar", out=MOD[:, ls, 8:16, :], in0=MOD[:, ls, 8:16, :], scalar1=1.0, scalar2=None, op0=ALU.add), reads=["MOD"], writes=["MOD"])
            P.op("dve", I("tensor_scalar", out=MOD[:, ls, 16:24, :], in0=MOD[:, ls, 16:24, :], scalar1=1.0, scalar2=1.0 / ALPHA, op0=ALU.add, op1=ALU.mult), reads=["MOD"], writes=["MOD"])
            for sq in range(4):
                for qq in range(4):
                    P.op("dve", I("tensor_copy", out=G16[:, ls, :, 4 * sq + qq], in_=MOD[:, ls, 16:24, 1 + sq]), reads=["MOD"], writes=["G16"])

        for pi in passes:
            NC = 1024 if pi == 0 else 1040
            tiles = [(0, 512), (512, 512)] + ([(1024, 16)] if pi == 1 else [])
            seqs = [(0, 0, 1024)] + ([(1 + i, 1024 + 4 * i, 4) for i in range(4)] if pi == 1 else [])
            tbase = 1024 * pi
            KT = KTA if pi == 0 else KTB
            VV = VA if pi == 0 else VB
            Xk = ["X%d" % c for c in range(8)]
            Hk = ["H%d" % c for c in range(8)]

            P.barrier()
            P.dma("sp", "d_x", I("dma_start", out=X[:, :, 0:1024], in_=xp[:, :, tbase:tbase + 1024]), writes=Xk)
            if pi == 1:
                P.dma("sp", "d_x", I("dma_start", out=X[:, :, 1024:1040], in_=xs), writes=Xk)

            def seq_ranges(c0, n):
                out = []
                for (si, s0, sn) in seqs:
                    a, b = max(c0, s0), min(c0 + n, s0 + sn)
                    if a < b:
                        out.append((si, a, b))
                return out

            def modulate(ls):
                for c in range(8):
                    for (si, s0, sn) in seqs:
                        P.op("act", I("activation", out=H[:, c, s0:s0 + sn], in_=X[:, c, s0:s0 + sn], func=AF.Identity, scale=MOD[:, ls, 8 + c, si:si + 1], bias=MOD[:, ls, c, si:si + 1]),
                             reads=[Xk[c], "MOD"], writes=[Hk[c]])

            def resid_add(ls, oc, c0, n, bk):
                if c0 == 1024:
                    kq = tmp16_rot["i"] % 2
                    tmp16_rot["i"] += 1
                    P.op("dve", I("tensor_tensor", out=TMP16[:, kq, :], in0=PS[bk][:, 0:16], in1=G16[:, ls, oc, :], op=ALU.mult),
                         reads=["ps%d" % bk, "G16"], writes=["TMP16_%d" % kq])
                    P.op("dve", I("tensor_tensor", out=X[:, oc, 1024:1040], in0=X[:, oc, 1024:1040], in1=TMP16[:, kq, :], op=ALU.add),
                         reads=["TMP16_%d" % kq, Xk[oc]], writes=[Xk[oc]])
                    return
                for (si, a, b) in seq_ranges(c0, n):
                    P.op("dve", I("scalar_tensor_tensor", out=X[:, oc, a:b], in0=PS[bk][:, a - c0:b - c0], scalar=MOD[:, ls, 16 + oc, si:si + 1], in1=X[:, oc, a:b], op0=ALU.mult, op1=ALU.add),
                         reads=["ps%d" % bk, "MOD", Xk[oc]], writes=[Xk[oc]])

            def dense_to_resid(tag_fn, ls, src, srck):
                for ob in range(2):
                    wv, wk = w_acquire(tag_fn(ob))
                    wv3 = wv[:, :].rearrange("p (k n) -> p k n", k=8)
                    for o4 in range(4):
                        oc = ob * 4 + o4
                        for (c0, n) in tiles:
                            bk = nb()
                            for kc in range(8):
                                P.op("pe", I("matmul", PS[bk][:, 0:n], lhsT=wv3[:, kc, o4 * 128:(o4 + 1) * 128], rhs=src[:, kc, c0:c0 + n], start=(kc == 0), stop=(kc == 7)),
                                     reads=[wk] + srck, writes=["ps%d" % bk], signal=(kc == 7))
                            resid_add(ls, oc, c0, n, bk)
                    w_release()

            def layer_norm(buf, bufk, gname, bname, vi, eps, outfn):
                P.barrier()
                for c in range(8):
                    tb, ts = TBs[c % 2], TSs[c % 2]
                    P.op("act", I("activation", out=tb[:, 0:NC], in_=buf[:, c, 0:NC], func=AF.Identity), reads=[bufk[c]], writes=["TB%d" % (c % 2)])
                    P.op("act", I("activation", out=ts[:, 0:NC], in_=buf[:, c, 0:NC], func=AF.Square), reads=[bufk[c]], writes=["TS%d" % (c % 2)])
                    for ti, (c0, n) in enumerate(tiles):
                        P.op("pe", I("matmul", PS[ti][:, 0:n], lhsT=ONESS[:, :], rhs=tb[:, c0:c0 + n], start=(c == 0), stop=(c == 7)),
                             reads=["TB%d" % (c % 2), "ONESS"], writes=["ps%d" % ti], signal=False)
                        P.op("pe", I("matmul", PS[3 + ti][:, 0:n], lhsT=ONESS[:, :], rhs=ts[:, c0:c0 + n], start=(c == 0), stop=(c == 7)),
                             reads=["TS%d" % (c % 2), "ONESS"], writes=["ps%d" % (3 + ti)], signal=(ti == len(tiles) - 1))
                for ti, (c0, n) in enumerate(tiles):
                    ms, rs, lt = MSs[ti], RSs[ti], LT[0]
                    P.op("act", I("activation", out=ms[:, 0:n], in_=PS[ti][:, 0:n], func=AF.Identity), reads=["ps%d" % ti], writes=["MS%d" % ti])
                    P.op("dve", I("tensor_tensor", out=lt[:, 0:n], in0=ms[:, 0:n], in1=ms[:, 0:n], op=ALU.mult), reads=["MS%d" % ti], writes=["LT0"])
                    P.op("dve", I("tensor_tensor", out=lt[:, 0:n], in0=PS[3 + ti][:, 0:n], in1=lt[:, 0:n], op=ALU.subtract), reads=["ps%d" % (3 + ti), "LT0"], writes=["LT0"])
                    P.op("dve", I("tensor_scalar", out=lt[:, 0:n], in0=lt[:, 0:n], scalar1=eps, scalar2=None, op0=ALU.add), reads=["LT0"], writes=["LT0"])
                    P.op("act", I("activation", out=lt[:, 0:n], in_=lt[:, 0:n], func=AF.Ln), reads=["LT0"], writes=["LT0"])
                    P.op("act", I("activation", out=rs[:, 0:n], in_=lt[:, 0:n], func=AF.Exp, scale=-0.5), reads=["LT0"], writes=["RS%d" % ti])
                k = 0
                for c in range(8):
                    for ti, (c0, n) in enumerate(tiles):
                        t = LT[k % 2]
                        tk = "LT%d" % (k % 2)
                        k += 1
                        P.op("dve", I("tensor_tensor", out=t[:, 0:n], in0=buf[:, c, c0:c0 + n], in1=MSs[ti][:, 0:n], op=ALU.subtract), reads=[bufk[c], "MS%d" % ti], writes=[tk])
                        P.op("dve", I("tensor_tensor", out=t[:, 0:n], in0=t[:, 0:n], in1=RSs[ti][:, 0:n], op=ALU.mult), reads=[tk, "RS%d" % ti], writes=[tk])
                        outfn(c, c0, n, t, tk)
                P.barrier()

            def post_ln(ls):
                def outfn(c, c0, n, t, tk):
                    P.op("act", I("activation", out=X[:, c, c0:c0 + n], in_=t[:, 0:n], func=AF.Identity, scale=V_("lng", ls * 8 + c), bias=V_("lnb", ls * 8 + c)),
                         reads=[tk, "VEC"], writes=[Xk[c]])
                layer_norm(X, Xk, "lng", "lnb", ls, LN_EPS / (ALPHA * ALPHA), outfn)

            for l in range(N_LAYERS_RUN):
                ls = 2 * l
                if l < 2:
                    modulate(ls)
                    YCk = ["YC%d" % c for c in range(8)]
                    for b in range(4):
                        wv, wk = w_acquire(("pw1", l, b))
                        wv3 = wv[:, :].rearrange("p (k n) -> p k n", k=8)
                        for cc in range(2):
                            c = 2 * b + cc
                            U, US = UB[c % 2], USB[c % 2]
                            Uk, USk = "U%d" % (c % 2), "US%d" % (c % 2)
                            if pi == 0:
                                P.op("dve", I("memset", U[:, 0:30], 0.0), writes=[Uk])
                            else:
                                P.op("dve", I("tensor_copy", out=U[:, 0:30], in_=TAIL[:, l, c, :]), reads=["TAIL"], writes=[Uk])
                                P.dma("sp", "d_us%d" % (c % 2), I("dma_start", out=US[:, :, 0:30], in_=stc[:, l, c, :, :]), writes=[USk])
                            for ti, (c0, n) in enumerate(tiles):
                                ba, bg = nb(), nb()
                                for kc in range(8):
                                    P.op("pe", I("matmul", PS[ba][:, 0:n], lhsT=wv3[:, kc, cc * 128:(cc + 1) * 128], rhs=H[:, kc, c0:c0 + n], start=(kc == 0), stop=(kc == 7)),
                                         reads=[wk] + Hk, writes=["ps%d" % ba], signal=False)
                                for kc in range(8):
                                    P.op("pe", I("matmul", PS[bg][:, 0:n], lhsT=wv3[:, kc, 256 + cc * 128:256 + (cc + 1) * 128], rhs=H[:, kc, c0:c0 + n], start=(kc == 0), stop=(kc == 7)),
                                         reads=[wk] + Hk, writes=["ps%d" % bg], signal=(kc == 7))
                                sg = SIG[ti % 2]
                                sgk = "SIG%d" % (ti % 2)
                                P.op("act", I("activation", out=sg[:, 0:n], in_=PS[bg][:, 0:n], func=AF.Sigmoid, bias=V_("pw1b", l * 16 + 8 + c)),
                                     reads=["ps%d" % bg, "VEC"], writes=[sgk])
                                if ti < 2:
                                    P.op("dve", I("scalar_tensor_tensor", out=U[:, 30 + c0:30 + c0 + n], in0=PS[ba][:, 0:n], scalar=V_("pw1b", l * 16 + c), in1=sg[:, 0:n], op0=ALU.add, op1=ALU.mult),
                                         reads=["ps%d" % ba, sgk, "VEC"], writes=[Uk])
                                else:
                                    P.op("dve", I("scalar_tensor_tensor", out=US[:, :, 30:34], in0=PS[ba][:, 0:16].rearrange("p (s t) -> p s t", s=4), scalar=V_("pw1b", l * 16 + c), in1=sg[:, 0:16].rearrange("p (s t) -> p s t", s=4), op0=ALU.add, op1=ALU.mult),
                                         reads=["ps%d" % ba, sgk, "VEC"], writes=[USk])
                            dwo = l * 248 + c * 31
                            P.op("dve", I("tensor_scalar", out=YC[:, c, 0:1024], in0=U[:, 0:1024], scalar1=V_("dww", dwo), scalar2=V_("dwb", l * 8 + c), op0=ALU.mult, op1=ALU.add),
                                 reads=[Uk, "VEC"], writes=[YCk[c]])
                            for j in range(1, 31):
                                P.op("dve", I("scalar_tensor_tensor", out=YC[:, c, 0:1024], in0=U[:, j:j + 1024], scalar=V_("dww", dwo + j), in1=YC[:, c, 0:1024], op0=ALU.mult, op1=ALU.add),
                                     reads=[Uk, "VEC", YCk[c]], writes=[YCk[c]])
                            if pi == 1:
                                ycs = YC[:, c, 1024:1040].rearrange("p (s t) -> p s t", s=4)
                                P.op("dve", I("tensor_scalar", out=ycs, in0=US[:, :, 0:4], scalar1=V_("dww", dwo), scalar2=V_("dwb", l * 8 + c), op0=ALU.mult, op1=ALU.add),
                                     reads=[USk, "VEC"], writes=[YCk[c]])
                                for j in range(1, 31):
                                    P.op("dve", I("scalar_tensor_tensor", out=ycs, in0=US[:, :, j:j + 4], scalar=V_("dww", dwo + j), in1=ycs, op0=ALU.mult, op1=ALU.add),
                                         reads=[USk, "VEC", YCk[c]], writes=[YCk[c]])
                            if pi == 0:
                                P.op("act", I("activation", out=TAIL[:, l, c, :], in_=U[:, 1024:1054], func=AF.Identity), reads=[Uk], writes=["TAIL"])
                            else:
                                P.dma("sp", "d_ncp%d" % (c % 2), I("dma_start", out=ncp[:, l, c, :], in_=U[:, 1024:1054]), reads=[Uk], writes=["o_ncp"])
                                P.dma("sp", "d_ncs%d" % (c % 2), I("dma_start", out=ncs[:, l, c, :, :], in_=US[:, :, 4:34]), reads=[USk], writes=["o_ncs"])
                        w_release()

                    def conv_out(c, c0, n, t, tk):
                        P.op("act", I("activation", out=H[:, c, c0:c0 + n], in_=t[:, 0:n], func=AF.Silu, scale=V_("clg", l * 8 + c), bias=V_("clb", l * 8 + c)),
                             reads=[tk, "VEC"], writes=[Hk[c]])
                    layer_norm(YC, YCk, "clg", "clb", l, LN_EPS, conv_out)
                    dense_to_resid(lambda ob: ("pw2", l, ob), ls, H, Hk)
                else:
                    j = l - 2
                    P.barrier()
                    if l == 2:
                        for c in range(8):
                            P.op("act", I("activation", out=H[:, c, 0:NC], in_=X[:, c, 0:NC], func=AF.Identity), reads=[Xk[c]], writes=[Hk[c]])
                        toks = [(tk * 128, 128, tk) for tk in range(8)] + ([(1024, 16, 8)] if pi == 1 else [])
                        for b in range(4):
                            wv, wk = w_acquire(("wkv", b))
                            wv3 = wv[:, :].rearrange("p (k n) -> p k n", k=8)
                            if b < 2:
                                for o4 in range(4):
                                    hp = b * 4 + o4
                                    for ti, (c0, n) in enumerate(tiles):
                                        bk = nb()
                                        for kc in range(8):
                                            P.op("pe", I("matmul", PS[bk][:, 0:n], lhsT=wv3[:, kc, o4 * 128:(o4 + 1) * 128], rhs=H[:, kc, c0:c0 + n], start=(kc == 0), stop=(kc == 7)),
                                                 reads=[wk] + Hk, writes=["ps%d" % bk], signal=(kc == 7))
                                        copy_op(evac_engine(), KT[:, hp, c0:c0 + n], PS[bk][:, 0:n], ["ps%d" % bk], ["KT%d_%d_%d" % (pi, hp, ti)])
                            for (t0, tn, tk) in toks:
                                bk = nb()
                                for kc in range(8):
                                    P.op("pe", I("matmul", PS[bk][0:tn, 0:512], lhsT=H[:, kc, t0:t0 + tn], rhs=wv3[:, kc, :], start=(kc == 0), stop=(kc == 7)),
                                         reads=[wk] + Hk, writes=["ps%d" % bk], signal=(kc == 7))
                                sgi = bank_rot["i"] % 2
                                sg = STG[sgi]
                                P.op("act", I("activation", out=sg[0:tn, :], in_=PS[bk][0:tn, :], func=AF.Identity), reads=["ps%d" % bk], writes=["STG%d" % sgi])
                                if tk < 8:
                                    dst = (nkp if b < 2 else nvp)[tbase + t0:tbase + t0 + 128, (b % 2) * 512:(b % 2) * 512 + 512]
                                else:
                                    dst = (nks if b < 2 else nvs)[0:16, (b % 2) * 512:(b % 2) * 512 + 512]
                                P.dma("sp", "d_stg%d" % sgi, I("dma_start", out=dst, in_=sg[0:tn, :]), reads=["STG%d" % sgi], writes=["o_kv"])
                                if b >= 2:
                                    if tk < 8:
                                        vd = VV[:, tk, (b - 2) * 512:(b - 2) * 512 + 512]
                                        vk = "V%d_%d" % (pi, tk)
                                    else:
                                        vd = VN[0:16, (b - 2) * 512:(b - 2) * 512 + 512]
                                        vk = "VN"
                                    P.op("dve", I("tensor_copy", out=vd, in_=sg[0:tn, :]), reads=["STG%d" % sgi], writes=[vk + "_%d" % b])
                            w_release()
                    modulate(ls)
                    for ob in range(2):
                        wv, wk = w_acquire(("wq", j, ob))
                        wv3 = wv[:, :].rearrange("p (k n) -> p k n", k=8)
                        for o4 in range(4):
                            hp = ob * 4 + o4
                            for (c0, n) in tiles:
                                bk = nb()
                                for kc in range(8):
                                    P.op("pe", I("matmul", PS[bk][:, 0:n], lhsT=wv3[:, kc, o4 * 128:(o4 + 1) * 128], rhs=H[:, kc, c0:c0 + n], start=(kc == 0), stop=(kc == 7)),
                                         reads=[wk] + Hk, writes=["ps%d" % bk], signal=(kc == 7))
                                copy_op(evac_engine(), Q[:, hp, c0:c0 + n], PS[bk][:, 0:n], ["ps%d" % bk], ["Q%d" % hp])
                        w_release()
                    P.barrier()

                    def kt_ap(hp, kc, pb):
                        if kc < 8:
                            return KTA[pb:pb + 64, hp, kc * 128:(kc + 1) * 128], "KT0_%d_%d" % (hp, kc // 4)
                        return KTB[pb:pb + 64, hp, (kc - 8) * 128:(kc - 7) * 128], "KT1_%d_%d" % (hp, (kc - 8) // 4)

                    def v_ap(kc, h):
                        if kc < 8:
                            return VA[:, kc, h * 64:(h + 1) * 64], ["V0_%d_%d" % (kc, 2 + h // 8)]
                        return VB[:, kc - 8, h * 64:(h + 1) * 64], ["V1_%d_%d" % (kc - 8, 2 + h // 8)]

                    for hp in (range(8) if do_prompt_attn else []):
                        qn = QN[hp % 2]
                        qnk = "QN%d" % (hp % 2)
                        P.op("dve", I("tensor_scalar", out=qn[:, 0:1024], in0=Q[:, hp, 0:1024], scalar1=-0.125, scalar2=None, op0=ALU.mult), reads=["Q%d" % hp], writes=[qnk])
                        for qi in range(2):
                            gq = 2 * pi + qi
                            qc = qi * 512
                            nch = 4 * gq + 4
                            kcs = list(reversed(range(nch)))

                            def st_Z(sx, i):
                                kc = kcs[i]
                                r = kc - 4 * gq
                                band = r >= 0
                                pb = sx * 64
                                cl = 128 * r if band else 0
                                kap, kk = kt_ap(hp, kc, pb)
                                P.op("pe", I("matmul", PS[sx][:, cl:512], lhsT=kap, rhs=Q[pb:pb + 64, hp, qc + cl:qc + 512], start=True, stop=(not band)),
                                     reads=[kk, "Q%d" % hp], writes=["ps%d" % sx], signal=(not band))
                                if band:
                                    P.op("pe", I("matmul", PS[sx][:, cl:cl + 128], lhsT=IDENT[:, :], rhs=MASK[:, 0, 0:128], start=False, stop=True),
                                         reads=["IDENT", "MASK"], writes=["ps%d" % sx])

                            def st_E(sx, i):
                                h = 2 * hp + sx
                                cl = max(0, 128 * (kcs[i] - 4 * gq))
                                bias_ap = BIAS[:, j * 16 + h:j * 16 + h + 1]
                                sp, spk = SPT[2 * sx + i % 2], "SP%d" % (2 * sx + i % 2)
                                P.op("act", I("activation", out=ET[sx][:, cl:512], in_=PS[sx][:, cl:512], func=AF.Exp, bias=bias_ap, scale=0.125),
                                     reads=["ps%d" % sx, "BIAS"], writes=["ET%d" % sx])
                                P.op("act", I("activation", out=sp[:, cl:512], in_=ET[sx][:, cl:512], func=AF.Ln, bias=1.0, scale=1.0),
                                     reads=["ET%d" % sx], writes=[spk])

                            def st_X(sx, i):
                                kc = kcs[i]
                                r = kc - 4 * gq
                                band = r >= 0
                                pb = sx * 64
                                xb = 2 + sx
                                cl = 128 * r if band else 0
                                kap, kk = kt_ap(hp, kc, pb)
                                sp, spk = SPT[2 * sx + i % 2], "SP%d" % (2 * sx + i % 2)
                                P.op("pe", I("matmul", PS[xb][:, cl:512], lhsT=TRI[:, :], rhs=sp[:, cl:512], start=True, stop=False),
                                     reads=["TRI", spk], writes=["ps%d" % xb], signal=False)
                                if i > 0:
                                    P.op("pe", I("matmul", PS[xb][:, cl:512], lhsT=ONES[:, :], rhs=STs[sx][:, cl:512], start=False, stop=False),
                                         reads=["ONES", "ST%d" % sx], writes=["ps%d" % xb], signal=False)
                                P.op("pe", I("matmul", PS[xb][:, cl:512], lhsT=kap, rhs=qn[pb:pb + 64, qc + cl:qc + 512], start=False, stop=(not band)),
                                     reads=[kk, qnk], writes=["ps%d" % xb], signal=(not band))
                                if band:
                                    P.op("pe", I("matmul", PS[xb][:, cl:cl + 128], lhsT=IDENTN[:, :], rhs=MASK[:, 0, 0:128], start=False, stop=True),
                                         reads=["IDENTN", "MASK"], writes=["ps%d" % xb])

                            def st_W(sx, i):
                                h = 2 * hp + sx
                                bias_ap = BIAS[:, j * 16 + h:j * 16 + h + 1]
                                xb = 2 + sx
                                cl = max(0, 128 * (kcs[i] - 4 * gq))
                                wt, wtk = WT[2 * sx + i % 2], "WT%d" % (2 * sx + i % 2)
                                P.op("act", I("activation", out=wt[:, cl:512], in_=PS[xb][:, cl:512], func=AF.Exp, bias=bias_ap, scale=-1.0),
                                     reads=["ps%d" % xb, "BIAS"], writes=[wtk])

                            def st_S(sx, i):
                                sp, spk = SPT[2 * sx + i % 2], "SP%d" % (2 * sx + i % 2)
                                cl = max(0, 128 * (kcs[i] - 4 * gq))
                                if i < nch - 1:
                                    if i == 0:
                                        if cl > 0:
                                            P.op("dve", I("memset", STs[sx][:, 0:cl], 0.0), writes=["ST%d" % sx])
                                        P.op("dve", I("tensor_copy", out=STs[sx][:, cl:512], in_=sp[:, cl:512]), reads=[spk], writes=["ST%d" % sx])
                                    else:
                                        P.op("dve", I("tensor_tensor", out=STs[sx][:, cl:512], in0=STs[sx][:, cl:512], in1=sp[:, cl:512], op=ALU.add), reads=[spk, "ST%d" % sx], writes=["ST%d" % sx])

                            def st_P(sx, i):
                                kc = kcs[i]
                                h = 2 * hp + sx
                                pb = sx * 64
                                ob_ = 4 + sx + 2 * (qi % 2)
                                wt, wtk = WT[2 * sx + i % 2], "WT%d" % (2 * sx + i % 2)
                                vap, vk = v_ap(kc, h)
                                cl = max(0, 128 * (kc - 4 * gq))
                                P.op("pe", I("matmul", PS[ob_][pb:pb + 64, cl:512], lhsT=vap, rhs=wt[:, cl:512], start=(i == 0), stop=(i == nch - 1), skip_group_check=True),
                                     reads=vk + [wtk], writes=["ps%d" % ob_])

                            for sx in range(2):
                                st_Z(sx, 0)
                            for sx in range(2):
                                st_E(sx, 0)
                            for i in range(nch):
                                for sx in range(2):
                                    st_X(sx, i)
                                if i + 1 < nch:
                                    for sx in range(2):
                                        st_Z(sx, i + 1)
                                for sx in range(2):
                                    st_W(sx, i)
                                for sx in range(2):
                                    st_S(sx, i)
                                for sx in range(2):
                                    st_P(sx, i)
                                if i + 1 < nch:
                                    for sx in range(2):
                                        st_E(sx, i + 1)
                            for sx in range(2):
                                pb = sx * 64
                                ob_ = 4 + sx + 2 * (qi % 2)
                                P.op("dve", I("tensor_copy", out=H[pb:pb + 64, hp, qc:qc + 512], in_=PS[ob_][pb:pb + 64, :]),
                                     reads=["ps%d" % ob_], writes=[Hk[hp]])
                    if pi == 1 and do_sample_attn:
                        P.barrier()
                        for pg in range(4):
                            for qq in range(4):
                                P.op("dve", I("tensor_copy", out=BM[:, j, pg, :].rearrange("p (h q) -> p h q", q=4)[:, :, qq], in_=BIAS[:, j * 16:(j + 1) * 16]), reads=["BIAS"], writes=["BM"])
                        gcount = 0
                        for i in range(4):
                            scol = 1024 + 4 * i
                            P.op("dve", I("tensor_copy", out=QB[0:64, :, 0:4], in_=Q[0:64, :, scol:scol + 4]), reads=["Q%d" % hp for hp in range(8)], writes=["QB"])
                            P.op("dve", I("tensor_copy", out=QB[64:128, :, 4:8], in_=Q[64:128, :, scol:scol + 4]), reads=["Q%d" % hp for hp in range(8)], writes=["QB"])
                            zb = 0
                            for hp in range(8):
                                P.op("pe", I("matmul", PS[0][0:16, hp * 8:hp * 8 + 8], lhsT=KTB[:, hp, 1024:1040], rhs=QB[:, hp, :], start=True, stop=True),
                                     reads=["KT1_%d_2" % hp, "QB"], writes=["ps0"], signal=(hp == 7))
                            P.op("dve", I("scalar_tensor_tensor", out=ZBt[0:16, 0:64], in0=PS[0][0:16, 0:64], scalar=0.125, in1=BM[0:16, j, 0, :], op0=ALU.mult, op1=ALU.add), reads=["ps0", "BM"], writes=["ZBt"])
                            P.op("dve", I("tensor_tensor", out=ZBt[0:16, 0:64], in0=ZBt[0:16, 0:64], in1=MASKN[:, i, :], op=ALU.add), reads=["ZBt", "MASKN"], writes=["ZBt"])
                            P.op("act", I("activation", out=ETs[0:16, 0:64], in_=ZBt[0:16, 0:64], func=AF.Exp), reads=["ZBt"], writes=["ETs"])
                            P.op("act", I("activation", out=SPb[0:16, 0:64], in_=ETs[0:16, 0:64], func=AF.Ln, bias=1.0, scale=1.0), reads=["ETs"], writes=["SPb"])
                            P.op("pe", I("matmul", PS[2][0:16, 0:64], lhsT=TRI[0:16, 0:16], rhs=SPb[0:16, 0:64], start=True, stop=True), reads=["TRI", "SPb"], writes=["ps2"])
                            P.op("dve", I("tensor_tensor", out=TMt[0:16, 0:64], in0=ZBt[0:16, 0:64], in1=PS[2][0:16, 0:64], op=ALU.subtract), reads=["ZBt", "ps2"], writes=["TMt"])
                            P.op("act", I("activation", out=Wb[0:16, 0:64], in_=TMt[0:16, 0:64], func=AF.Exp), reads=["TMt"], writes=["Wb"])
                            for hp in range(8):
                                P.op("pe", I("matmul", PS[4][:, hp * 8:hp * 8 + 8], lhsT=VN[0:16, hp * 128:(hp + 1) * 128], rhs=Wb[0:16, hp * 8:hp * 8 + 8], start=True, stop=True),
                                     reads=["VN_2", "VN_3", "Wb"], writes=["ps4"], signal=(hp == 7))
                            P.op("dve", I("tensor_copy", out=OACC[:, :], in_=PS[4][:, 0:64]), reads=["ps4"], writes=["OACC"])
                            P.op("dve", I("memset", SC[:, :], 0.0), writes=["SC"])
                            P.op("dve", I("tensor_copy", out=SC[0:16, :], in_=SPb[0:16, 0:64]), reads=["SPb"], writes=["SC"])
                            TP = 4
                            TW = TP * 64
                            for t in reversed(range(64 // TP)):
                                zbk = t % 2
                                xbk = 2 + t % 2
                                for pg in reversed(range(TP)):
                                    page = TP * t + pg
                                    ks = gcount % 3
                                    kts = 0
                                    gcount += 1
                                    kb = KBs[ks]
                                    KTp = KTps[kts]
                                    vs = (t % 2) * TP + pg
                                    col = i * 64 + page
                                    P.dma("pool", "d_kb%d" % ks, I("indirect_dma_start", out=kb[:, :], out_offset=None, in_=ck, in_offset=bass.IndirectOffsetOnAxis(ap=IDX[:, col:col + 1], axis=0)),
                                          reads=["IDX"], writes=["KB%d" % ks])
                                    P.dma("pool", "d_vb%d" % vs, I("indirect_dma_start", out=VBs[vs][:, :], out_offset=None, in_=cv, in_offset=bass.IndirectOffsetOnAxis(ap=IDX[:, col:col + 1], axis=0)),
                                          reads=["IDX"], writes=["VB%d" % vs])
                                    for hp in range(8):
                                        P.op("pe", I("transpose", out=PST[:, hp * 128:(hp + 1) * 128], in_=kb[:, hp * 128:(hp + 1) * 128], identity=IDENT[:, :]),
                                             reads=["KB%d" % ks, "IDENT"], writes=["ps7"], signal=(hp == 7))
                                    P.op("act", I("activation", out=KTp[:, :], in_=PST[:, :], func=AF.Identity), reads=["ps7"], writes=["KTp%d" % kts])
                                    for hp in range(8):
                                        P.op("pe", I("matmul", PS[zbk][:, pg * 64 + hp * 8:pg * 64 + hp * 8 + 8], lhsT=KTp[:, hp * 128:(hp + 1) * 128], rhs=QB[:, hp, :], start=True, stop=True),
                                             reads=["KTp%d" % kts, "QB"], writes=["ps%d" % zbk], signal=(hp == 7))
                                P.op("dve", I("scalar_tensor_tensor", out=ZBt[:, 0:TW], in0=PS[zbk][:, 0:TW], scalar=0.125, in1=BM[:, j, :, :].rearrange("p a b -> p (a b)"), op0=ALU.mult, op1=ALU.add), reads=["ps%d" % zbk, "BM"], writes=["ZBt"])
                                P.op("act", I("activation", out=ETs[:, 0:TW], in_=ZBt[:, 0:TW], func=AF.Exp), reads=["ZBt"], writes=["ETs"])
                                P.op("act", I("activation", out=SPb[:, 0:TW], in_=ETs[:, 0:TW], func=AF.Ln, bias=1.0, scale=1.0), reads=["ETs"], writes=["SPb"])
                                P.op("dve", I("tensor_copy", out=Stl[:, (TP - 1) * 64:TP * 64], in_=SC[:, :]), reads=["SC"], writes=["Stl"])
                                for pg in reversed(range(TP - 1)):
                                    P.op("dve", I("tensor_tensor", out=Stl[:, pg * 64:(pg + 1) * 64], in0=Stl[:, (pg + 1) * 64:(pg + 2) * 64], in1=SPb[:, (pg + 1) * 64:(pg + 2) * 64], op=ALU.add), reads=["Stl", "SPb"], writes=["Stl"])
                                P.op("dve", I("tensor_tensor", out=SC[:, :], in0=Stl[:, 0:64], in1=SPb[:, 0:64], op=ALU.add), reads=["Stl", "SPb"], writes=["SC"])
                                P.op("dve", I("tensor_copy", out=Sb[:, 0:TW], in_=Stl[:, 0:TW]), reads=["Stl"], writes=["Sb"])
                                P.op("pe", I("matmul", PS[xbk][:, 0:TW], lhsT=TRI[:, :], rhs=SPb[:, 0:TW], start=True, stop=False), reads=["TRI", "SPb"], writes=["ps%d" % xbk], signal=False)
                                P.op("pe", I("matmul", PS[xbk][:, 0:TW], lhsT=ONES[:, :], rhs=Sb[:, 0:TW], start=False, stop=True), reads=["ONES", "Sb"], writes=["ps%d" % xbk])
                                P.op("dve", I("tensor_tensor", out=TMt[:, 0:TW], in0=ZBt[:, 0:TW], in1=PS[xbk][:, 0:TW], op=ALU.subtract), reads=["ZBt", "ps%d" % xbk], writes=["TMt"])
                                P.op("act", I("activation", out=Wb[:, 0:TW], in_=TMt[:, 0:TW], func=AF.Exp), reads=["TMt"], writes=["Wb"])
                                for hp in range(8):
                                    for pg in range(TP):
                                        vs = (t % 2) * TP + pg
                                        P.op("pe", I("matmul", PS[4][:, hp * 8:hp * 8 + 8], lhsT=VBs[vs][:, hp * 128:(hp + 1) * 128], rhs=Wb[:, pg * 64 + hp * 8:pg * 64 + hp * 8 + 8], start=(pg == 0), stop=(pg == TP - 1)),
                                             reads=["VB%d" % vs, "Wb"], writes=["ps4"], signal=(hp == 7 and pg == TP - 1))
                                P.op("dve", I("tensor_tensor", out=OACC[:, :], in0=OACC[:, :], in1=PS[4][:, 0:64], op=ALU.add), reads=["ps4", "OACC"], writes=["OACC"])
                            oa = OACC[:, :].rearrange("p (h c) -> p h c", c=8)
                            P.op("dve", I("tensor_copy", out=H[0:64, :, scol:scol + 4], in_=oa[0:64, :, 0:4]), reads=["OACC"], writes=Hk)
                            P.op("dve", I("tensor_copy", out=H[64:128, :, scol:scol + 4], in_=oa[64:128, :, 4:8]), reads=["OACC"], writes=Hk)
                    P.barrier()
                    dense_to_resid(lambda ob: ("wo", j, ob), ls, H, Hk)
                post_ln(ls)

                ls = 2 * l + 1
                modulate(ls)
                k = 0
                for jb in range(8):
                    wv, wk = w_acquire(("w1", l, jb))
                    wv3 = wv[:, :].rearrange("p (k n) -> p k n", k=8)
                    hid, hidk = HID[jb % 2], "HID%d" % (jb % 2)
                    for hc in range(4):
                        for (c0, n) in tiles:
                            bk = nb()
                            for kc in range(8):
                                P.op("pe", I("matmul", PS[bk][:, 0:n], lhsT=wv3[:, kc, hc * 128:(hc + 1) * 128], rhs=H[:, kc, c0:c0 + n], start=(kc == 0), stop=(kc == 7)),
                                     reads=[wk] + Hk, writes=["ps%d" % bk], signal=(kc == 7))
                            rl, rlk = RL[k % 2], "RL%d" % (k % 2)
                            k += 1
                            P.op("act", I("activation", out=rl[:, 0:n], in_=PS[bk][:, 0:n], func=AF.Relu), reads=["ps%d" % bk], writes=[rlk])
                            P.op("act", I("activation", out=hid[:, hc, c0:c0 + n], in_=rl[:, 0:n], func=AF.Square), reads=[rlk], writes=[hidk])
                    w_release()
                    wv, wk = w_acquire(("w2", l, jb))
                    wv4 = wv[:, :].rearrange("p (k n) -> p k n", k=4)
                    for oc in range(8):
                        for (c0, n) in tiles:
                            bk = nb()
                            for hc in range(4):
                                P.op("pe", I("matmul", PS[bk][:, 0:n], lhsT=wv4[:, hc, oc * 128:(oc + 1) * 128], rhs=hid[:, hc, c0:c0 + n], start=(hc == 0), stop=(hc == 3)),
                                     reads=[wk, hidk], writes=["ps%d" % bk], signal=(hc == 3))
                            resid_add(ls, oc, c0, n, bk)
                    w_release()
                post_ln(ls)

            P.dma("sp", "d_y", I("dma_start", out=yp[:, :, tbase:tbase + 1024], in_=X[:, :, 0:1024]), reads=Xk, writes=["o_y"])
            if pi == 1:
                P.dma("sp", "d_y", I("dma_start", out=ys, in_=X[:, :, 1024:1040]), reads=Xk, writes=["o_ys"])
        assert wst["use"] == len(plan), (wst["use"], len(plan))
        P.emit()
    return nc


def _fm(v):
    v = np.asarray(v, np.float32)
    lead = v.shape[:-1]
    w = v.reshape(lead + (8, 128))
    return np.ascontiguousarray(np.moveaxis(w, -1, 0))


_NC_CACHE = {}


def _prep(x_prompt, x_sample, c_prompt, c_sample, state_conv, cache_k, cache_v, page_table,
          ada_w, ada_b, ln_g, ln_b, mlp_w1, mlp_w2, conv_pw1_w, conv_pw1_b, conv_dw_w, conv_dw_b,
          conv_ln_g, conv_ln_b, conv_pw2_w, w_kv, attn_wq, attn_wo, attn_bias, cores=range(8)):
    f = np.float32
    npool = cache_k.shape[0]
    ck = np.ascontiguousarray(np.asarray(cache_k, f).reshape(npool * 128, 1024))
    cv = np.ascontiguousarray(np.asarray(cache_v, f).reshape(npool * 128, 1024))
    adaw = np.ascontiguousarray(np.asarray(ada_w, f).reshape(8, 1024, 3072))
    shared = dict(
        ck=ck, cv=cv, adaw=adaw,
        w1=np.ascontiguousarray(np.asarray(mlp_w1, f)), w2=np.ascontiguousarray(np.asarray(mlp_w2, f)),
        pw1=np.ascontiguousarray(np.asarray(conv_pw1_w, f)), pw2=np.ascontiguousarray(np.asarray(conv_pw2_w, f)),
        wkv=np.ascontiguousarray(np.asarray(w_kv, f)), wq=np.ascontiguousarray(np.asarray(attn_wq, f)),
        wo=np.ascontiguousarray(np.asarray(attn_wo, f)),
        abias=np.ascontiguousarray(np.asarray(attn_bias, f).reshape(1, 32)),
    )
    vec_common = np.zeros((128, NVEC), f)
    ab = np.asarray(ada_b, f).reshape(8, 24, 128)
    vec_common[:, VOFF["adab"]:VOFF["adab"] + 192] = ab.transpose(2, 0, 1).reshape(128, 192)
    vec_common[:, VOFF["lng"]:VOFF["lng"] + 64] = _fm(np.asarray(ln_g, f).reshape(8, 1024)).reshape(128, 64)
    vec_common[:, VOFF["lnb"]:VOFF["lnb"] + 64] = _fm(np.asarray(ln_b, f).reshape(8, 1024)).reshape(128, 64)
    pb = np.asarray(conv_pw1_b, f).reshape(2, 16, 128)
    vec_common[:, VOFF["pw1b"]:VOFF["pw1b"] + 32] = pb.transpose(2, 0, 1).reshape(128, 32)
    dw = np.asarray(conv_dw_w, f).reshape(2, 31, 8, 128)
    vec_common[:, VOFF["dww"]:VOFF["dww"] + 496] = dw.transpose(3, 0, 2, 1).reshape(128, 496)
    vec_common[:, VOFF["dwb"]:VOFF["dwb"] + 16] = _fm(np.asarray(conv_dw_b, f)).reshape(128, 16)
    vec_common[:, VOFF["clg"]:VOFF["clg"] + 16] = _fm(np.asarray(conv_ln_g, f)).reshape(128, 16)
    vec_common[:, VOFF["clb"]:VOFF["clb"] + 16] = _fm(np.asarray(conv_ln_b, f)).reshape(128, 16)
    xpr = np.asarray(x_prompt, f)
    xsa = np.asarray(x_sample, f)
    stv = np.asarray(state_conv, f)
    in_maps = []
    for c in cores:
        m = dict(shared)
        m["xp"] = np.ascontiguousarray(xpr[c].T.reshape(8, 128, 2048).transpose(1, 0, 2))
        m["xs"] = np.ascontiguousarray(xsa[4 * c:4 * c + 4].reshape(16, 1024).T.reshape(8, 128, 16).transpose(1, 0, 2))
        v = vec_common.copy()
        cc = np.concatenate([np.asarray(c_prompt, f)[c:c + 1], np.asarray(c_sample, f)[4 * c:4 * c + 4]], 0)
        v[:, VOFF["ct"]:VOFF["ct"] + 40] = cc.reshape(5, 8, 128).transpose(2, 1, 0).reshape(128, 40)
        m["vec"] = v
        s = stv[:, 4 * c:4 * c + 4].reshape(2, 4, 30, 8, 128)
        m["stc"] = np.ascontiguousarray(s.transpose(4, 0, 3, 1, 2))
        m["ptab"] = np.ascontiguousarray(np.asarray(page_table, np.int32)[4 * c:4 * c + 4])
        in_maps.append(m)
    return in_maps


def _assemble(res):
    f = np.float32

    def unfm(a):
        a = np.asarray(a)
        return np.ascontiguousarray(a.transpose(2, 1, 0).reshape(a.shape[2], 1024))

    y_prompt = np.stack([unfm(r["yp"]) for r in res], 0)
    y_sample = np.concatenate([unfm(r["ys"]).reshape(4, 4, 1024) for r in res], 0)
    ncp_ = np.stack([np.asarray(r["ncp"]).transpose(1, 3, 2, 0).reshape(2, 30, 1024) for r in res], 1)
    ncs_ = np.concatenate([np.asarray(r["ncs"]).transpose(1, 3, 4, 2, 0).reshape(2, 4, 30, 1024) for r in res], 1)
    nk_p = np.stack([np.asarray(r["nkp"]).reshape(2048, 16, 64) for r in res], 0)
    nv_p = np.stack([np.asarray(r["nvp"]).reshape(2048, 16, 64) for r in res], 0)
    nk_s = np.concatenate([np.asarray(r["nks"]).reshape(4, 4, 16, 64) for r in res], 0)
    nv_s = np.concatenate([np.asarray(r["nvs"]).reshape(4, 4, 16, 64) for r in res], 0)
    return (y_prompt.astype(f), y_sample.astype(f), np.ascontiguousarray(ncp_).astype(f),
            np.ascontiguousarray(ncs_).astype(f), nk_p.astype(f), nv_p.astype(f), nk_s.astype(f), nv_s.astype(f))


def kernel(**inputs):
    if "nc" not in _NC_CACHE:
        _NC_CACHE["nc"] = build_program()
    nc = _NC_CACHE["nc"]
    in_maps = _prep(**inputs)
    res = run_bass_kernel_spmd(nc, in_maps, core_ids=list(range(8))).results
    return _assemble(res)
```
